# Optimizing a Trainium2 kernel written in Bass

```python
import math
import jax, jax.numpy as jnp
from jax import lax
import numpy as np

D_MODEL = 1024
BATCH = 16
SEQ = 256
DEPTH = 4
DEC_BATCH = 2
DEC_SEQ = 4096
PAST_LEN = 256

GRID_W = 64
EPS = 1e-6
H_Q = 8
H_KV = 2
HEAD_DIM = 128
ATTN_W = H_Q * HEAD_DIM
KV_W = H_KV * HEAD_DIM
ROPE_THETA = 10000.0
Q_BLOCK = 128
D_INNER = 2 * D_MODEL
SSM_HEAD_DIM = 64
SSM_HEADS = D_INNER // SSM_HEAD_DIM
SSM_GROUPS = 4
SSM_STATE = 128
SSM_CONV = 3
SSM_CHUNK = 128
CONV_CH = D_INNER + 2 * SSM_GROUPS * SSM_STATE
D_FF = 2816
FFN_CONV = 3
IN_SIZES = (ATTN_W, KV_W, KV_W, D_INNER, CONV_CH, 2 * SSM_HEADS, D_MODEL, D_MODEL)
IN_W = ATTN_W + 2 * KV_W + D_INNER + CONV_CH + 2 * SSM_HEADS + 2 * D_MODEL

kernel_name = 'hybrid_dit_gqa_ssd_convffn_step'


def rms_norm(x):
    xf = x.astype(jnp.float32)
    return (xf * lax.rsqrt(jnp.mean(xf * xf, axis=-1, keepdims=True) + EPS)).astype(x.dtype)


def modulate(x, shift, scale):
    return rms_norm(x) * (1.0 + scale) + shift


def dw_conv(x, w, b):
    width = w.shape[0]
    pad = width // 2
    seq = x.shape[1]
    xp = jnp.pad(x, ((0, 0), (pad, width - 1 - pad), (0, 0)))
    out = b
    for i in range(width):
        out = out + xp[:, i:i + seq] * w[i]
    return out


def axial_rope_tables(seq):
    rows = seq // GRID_W
    row = jnp.repeat(jnp.arange(rows), GRID_W).astype(jnp.float32)
    col = jnp.tile(jnp.arange(GRID_W), rows).astype(jnp.float32)
    half = HEAD_DIM // 2
    inv_freq = ROPE_THETA ** (-jnp.arange(0, half, 2, dtype=jnp.float32) / half)
    ang_r = row[:, None] * inv_freq
    ang_c = col[:, None] * inv_freq
    ang = jnp.concatenate([ang_r, ang_r, ang_c, ang_c], axis=-1)
    return jnp.cos(ang), jnp.sin(ang)


def rotate_half(u):
    u1, u2 = jnp.split(u, 2, axis=-1)
    return jnp.concatenate([-u2, u1], axis=-1)


def apply_axial_rope(x, cos, sin):
    xr, xc = jnp.split(x, 2, axis=-1)
    xrot = jnp.concatenate([rotate_half(xr), rotate_half(xc)], axis=-1)
    return (x * cos[None, :, None, :] + xrot * sin[None, :, None, :]).astype(x.dtype)


def block_attention(q, k, v):
    b, lq = q.shape[0], q.shape[1]
    rep = H_Q // H_KV
    nb = lq // Q_BLOCK
    qb = jnp.moveaxis(q.reshape(b, nb, Q_BLOCK, H_KV, rep, HEAD_DIM), 1, 0)
    scale = HEAD_DIM ** -0.5

    def one_block(qi):
        s = jnp.einsum('bqgrd,bkgd->bgrqk', qi, k).astype(jnp.float32) * scale
        p = jax.nn.softmax(s, axis=-1).astype(v.dtype)
        return jnp.einsum('bgrqk,bkgd->bqgrd', p, v)

    o = lax.map(one_block, qb)
    return jnp.moveaxis(o, 0, 1).reshape(b, lq, ATTN_W)


def ssd_scan(x, dt, a, bm, cm, h0):
    b, seq = x.shape[0], x.shape[1]
    nc = seq // SSM_CHUNK
    qc = SSM_CHUNK
    g = SSM_GROUPS
    r = SSM_HEADS // SSM_GROUPS
    f32 = jnp.float32
    x = x.astype(f32).reshape(b, nc, qc, g, r, SSM_HEAD_DIM)
    dt = dt.astype(f32).reshape(b, nc, qc, g, r)
    bm = bm.astype(f32).reshape(b, nc, qc, g, SSM_STATE)
    cm = cm.astype(f32).reshape(b, nc, qc, g, SSM_STATE)
    cum = jnp.cumsum(dt * a.reshape(g, r), axis=2)
    tri = jnp.tril(jnp.ones((qc, qc), dtype=bool))[None, None, :, :, None, None]
    seg = cum[:, :, :, None] - cum[:, :, None]
    decay = jnp.exp(jnp.where(tri, seg, -jnp.inf))
    cb = jnp.einsum('bclgn,bcsgn->bclsg', cm, bm)
    wts = cb[..., None] * decay * dt[:, :, None]
    y_diag = jnp.einsum('bclsgr,bcsgrp->bclgrp', wts, x)
    decay_s = jnp.exp(cum[:, :, -1:] - cum)
    states = jnp.einsum('bcsgn,bcsgrp->bcgrpn', bm, (decay_s * dt)[..., None] * x)
    chunk_decay = jnp.exp(cum[:, :, -1])
    h_init = h0.astype(f32).reshape(b, g, r, SSM_HEAD_DIM, SSM_STATE)

    def step(h, inp):
        dec, st = inp
        return h * dec[..., None, None] + st, h

    h_last, h_prev = lax.scan(step, h_init, (jnp.moveaxis(chunk_decay, 1, 0), jnp.moveaxis(states, 1, 0)))
    h_prev = jnp.moveaxis(h_prev, 0, 1)
    y_off = jnp.einsum('bclgn,bcgrpn->bclgrp', cm, h_prev) * jnp.exp(cum)[..., None]
    y = (y_diag + y_off).reshape(b, seq, SSM_HEADS, SSM_HEAD_DIM)
    return y, h_last.reshape(b, SSM_HEADS, SSM_HEAD_DIM, SSM_STATE)


def ssd_branch(xbc, z, dt_raw, dt_bias, a_log, d_skip, norm_w, init):
    b, seq = xbc.shape[0], xbc.shape[1]
    gn = SSM_GROUPS * SSM_STATE
    xs, bm, cm = jnp.split(xbc, [D_INNER, D_INNER + gn], axis=-1)
    xh = xs.reshape(b, seq, SSM_HEADS, SSM_HEAD_DIM)
    bm = bm.reshape(b, seq, SSM_GROUPS, SSM_STATE)
    cm = cm.reshape(b, seq, SSM_GROUPS, SSM_STATE)
    dt = jax.nn.softplus(dt_raw.astype(jnp.float32).reshape(b, seq, 2, SSM_HEADS) + dt_bias)
    a = -jnp.exp(a_log.astype(jnp.float32))
    y_f, h_f = ssd_scan(xh, dt[:, :, 0], a[0], bm, cm, init[:, 0])
    y_b, h_b = ssd_scan(jnp.flip(xh, 1), jnp.flip(dt[:, :, 1], 1), a[1], jnp.flip(bm, 1), jnp.flip(cm, 1), init[:, 1])
    y = y_f + jnp.flip(y_b, 1) + d_skip[:, None] * xh.astype(jnp.float32)
    y = y.reshape(b, seq, D_INNER) * jax.nn.silu(z.astype(jnp.float32))
    y = rms_norm(y) * norm_w
    return y.astype(xbc.dtype), jnp.stack([h_f, h_b], axis=1).astype(xbc.dtype)


def trunk_layer(x, mod, rope, ctx_k, ctx_v, init_state, p):
    b, seq = x.shape[0], x.shape[1]
    sh1, sc1, g1, sh2, sc2, g2 = jnp.split(mod[:, None, :], 6, axis=-1)
    h = modulate(x, sh1, sc1)
    proj = h @ p['w_in']
    q, k, v, z, xbc, dt_raw, ga, gs = jnp.split(proj, list(np.cumsum(IN_SIZES)[:-1]), axis=-1)
    q = rms_norm(q.reshape(b, seq, H_Q, HEAD_DIM)) * p['q_norm']
    k = rms_norm(k.reshape(b, seq, H_KV, HEAD_DIM)) * p['k_norm']
    v = v.reshape(b, seq, H_KV, HEAD_DIM)
    if rope is None:
        k_all, v_all = k, v
    else:
        cos, sin = rope
        q = apply_axial_rope(q, cos, sin)
        k = apply_axial_rope(k, cos, sin)
        k_all = jnp.concatenate([ctx_k.astype(k.dtype), k], axis=1)
        v_all = jnp.concatenate([ctx_v.astype(v.dtype), v], axis=1)
    attn = block_attention(q, k_all, v_all)
    xbc = jax.nn.silu(dw_conv(xbc, p['conv_w'], p['conv_b']))
    ssd, h_final = ssd_branch(xbc, z, dt_raw, p['dt_bias'], p['a_log'], p['d_skip'], p['ssd_norm'], init_state)
    merged = jax.nn.sigmoid(ga) * (attn @ p['w_attn_o']) + jax.nn.sigmoid(gs) * (ssd @ p['w_ssd_o'])
    x = x + g1 * (merged @ p['w_out'])
    h = modulate(x, sh2, sc2)
    u = dw_conv(h @ p['w_up'], p['ffn_conv_w'], p['ffn_conv_b'])
    u_val, u_gate = jnp.split(u, 2, axis=-1)
    x = x + g2 * ((jax.nn.silu(u_gate) * u_val) @ p['w_down'])
    return x, k, v, h_final


def setup_inputs(seed: int = 0) -> dict:
    key = jax.random.key(seed)
    ks = jax.random.split(key, 26)
    f32 = jnp.float32

    def nrm(k, shape, s):
        return jax.random.normal(k, shape, f32) * s

    dt0 = jnp.exp(jax.random.uniform(ks[14], (DEPTH, 2, SSM_HEADS), f32, math.log(1e-3), math.log(1e-1)))
    return {
        'x_prompt': nrm(ks[0], (BATCH, SEQ, D_MODEL), 1.0),
        'x_sample': nrm(ks[1], (DEC_BATCH, DEC_SEQ, D_MODEL), 1.0),
        'c': nrm(ks[2], (DEC_BATCH, D_MODEL), 1.0),
        'cache_k': nrm(ks[3], (DEC_BATCH, DEPTH, PAST_LEN, H_KV, HEAD_DIM), 1.0),
        'cache_v': nrm(ks[4], (DEC_BATCH, DEPTH, PAST_LEN, H_KV, HEAD_DIM), 1.0),
        'state_ssd': nrm(ks[5], (DEC_BATCH, DEPTH, 2, SSM_HEADS, SSM_HEAD_DIM, SSM_STATE), 0.5),
        'c_ctx': nrm(ks[6], (D_MODEL,), 1.0),
        'w_mod': nrm(ks[7], (DEPTH, D_MODEL, 6 * D_MODEL), D_MODEL ** -0.5),
        'b_mod': nrm(ks[8], (DEPTH, 6 * D_MODEL), 0.01),
        'w_in': nrm(ks[9], (DEPTH, D_MODEL, IN_W), D_MODEL ** -0.5),
        'q_norm': 1.0 + nrm(ks[10], (DEPTH, HEAD_DIM), 0.02),
        'k_norm': 1.0 + nrm(ks[11], (DEPTH, HEAD_DIM), 0.02),
        'conv_w': nrm(ks[12], (DEPTH, SSM_CONV, CONV_CH), SSM_CONV ** -0.5),
        'conv_b': nrm(ks[13], (DEPTH, CONV_CH), 0.01),
        'dt_bias': dt0 + jnp.log(-jnp.expm1(-dt0)),
        'a_log': jnp.log(jax.random.uniform(ks[15], (DEPTH, 2, SSM_HEADS), f32, 1.0, 16.0)),
        'd_skip': 1.0 + nrm(ks[16], (DEPTH, SSM_HEADS), 0.02),
        'ssd_norm': 1.0 + nrm(ks[17], (DEPTH, D_INNER), 0.02),
        'w_attn_o': nrm(ks[18], (DEPTH, ATTN_W, D_MODEL), ATTN_W ** -0.5),
        'w_ssd_o': nrm(ks[19], (DEPTH, D_INNER, D_MODEL), D_INNER ** -0.5),
        'w_out': nrm(ks[20], (DEPTH, D_MODEL, D_MODEL), D_MODEL ** -0.5),
        'w_up': nrm(ks[21], (DEPTH, D_MODEL, 2 * D_FF), D_MODEL ** -0.5),
        'ffn_conv_w': nrm(ks[22], (DEPTH, FFN_CONV, 2 * D_FF), FFN_CONV ** -0.5),
        'ffn_conv_b': nrm(ks[23], (DEPTH, 2 * D_FF), 0.01),
        'w_down': nrm(ks[24], (DEPTH, D_FF, D_MODEL), D_FF ** -0.5),
    }


def reference(x_prompt, x_sample, c, cache_k, cache_v, state_ssd, c_ctx, w_mod, b_mod, w_in, q_norm, k_norm,
              conv_w, conv_b, dt_bias, a_log, d_skip, ssd_norm, w_attn_o, w_ssd_o, w_out, w_up,
              ffn_conv_w, ffn_conv_b, w_down):
    rope = axial_rope_tables(x_sample.shape[1])
    xp = x_prompt
    xs = x_sample
    zero_state = jnp.zeros((xp.shape[0], 2, SSM_HEADS, SSM_HEAD_DIM, SSM_STATE), xp.dtype)
    new_k, new_v, new_s = [], [], []
    for l in range(DEPTH):
        p = {
            'w_in': w_in[l], 'q_norm': q_norm[l], 'k_norm': k_norm[l],
            'conv_w': conv_w[l], 'conv_b': conv_b[l], 'dt_bias': dt_bias[l], 'a_log': a_log[l],
            'd_skip': d_skip[l], 'ssd_norm': ssd_norm[l], 'w_attn_o': w_attn_o[l], 'w_ssd_o': w_ssd_o[l],
            'w_out': w_out[l], 'w_up': w_up[l], 'ffn_conv_w': ffn_conv_w[l], 'ffn_conv_b': ffn_conv_b[l],
            'w_down': w_down[l],
        }
        mod_ctx = jax.nn.silu(c_ctx)[None, :] @ w_mod[l] + b_mod[l]
        mod_lat = jax.nn.silu(c) @ w_mod[l] + b_mod[l]
        xp, k_l, v_l, s_l = trunk_layer(xp, mod_ctx, None, None, None, zero_state, p)
        new_k.append(k_l)
        new_v.append(v_l)
        new_s.append(s_l)
        xs, _, _, _ = trunk_layer(xs, mod_lat, rope, cache_k[:, l], cache_v[:, l], state_ssd[:, l], p)
    new_cache_k = jnp.stack(new_k, axis=1)
    new_cache_v = jnp.stack(new_v, axis=1)
    new_state_ssd = jnp.stack(new_s, axis=1)
    return (xp, xs, new_cache_k, new_cache_v, new_state_ssd)
```

```python
import numpy as np
import ml_dtypes
from contextlib import ExitStack
import concourse.bass as bass
import concourse.mybir as mybir
from concourse.bass_utils import run_bass_kernel_spmd

F32 = mybir.dt.float32
BF16 = mybir.dt.bfloat16
AF = mybir.ActivationFunctionType
ALU = mybir.AluOpType
SEM_LIMIT = 30000

DEPTH = 4
T = 4608
NT = 9
NB = 36
EPS = 1e-6
SEQS = [(0, 4096, 0), (4096, 256, 1), (4352, 256, 2)]
RAWN = 4612
SEGS = [(0, 0, 4096), (4096, 4097, 256), (4352, 4354, 256)]


class Res:
    __slots__ = ("name", "ws", "rs")

    def __init__(self, name):
        self.name = name
        self.ws = {}
        self.rs = {}


class Slot:
    __slots__ = ("name", "last", "n", "pool", "base", "kind")

    def __init__(self, name, kind):
        self.name = name
        self.kind = kind
        self.last = None
        self.n = 0
        self.pool = None
        self.base = 0


class Op:
    __slots__ = ("eng", "fn", "deps", "signal", "cnt", "slot", "stream", "sz")


class GlobalSems:
    def __init__(self, nc, es, n_eng=4, n_dma=72):
        self.engs = ("pe", "act", "dve", "pool", "sp")
        self.esem = {e: [es.enter_context(nc.semaphore(f"s_{e}{i}")) for i in range(n_eng)] for e in self.engs}
        self.ecnt = {e: 0 for e in self.engs}
        self.dsem = [es.enter_context(nc.semaphore(f"d{i}")) for i in range(n_dma)]
        self.dval = [0] * n_dma
        self.dkind = ["hw" if i < n_dma // 2 else "sw" for i in range(n_dma)]


class Sched:
    ENGS = ("pe", "act", "dve", "pool", "sp")

    def __init__(self, nc, gs):
        self.nc = nc
        self.gs = gs
        self.ops = {e: [] for e in self.ENGS}
        self.slots = []

    def slot(self, name, kind):
        s = Slot(name, kind)
        self.slots.append(s)
        return s

    def add(self, eng, fn, reads=(), writes=(), slot=None, sz=0):
        op = Op()
        op.sz = sz
        op.eng = eng
        op.fn = fn
        op.signal = False
        op.cnt = None
        op.slot = slot
        op.stream = slot if slot is not None else eng
        deps = {}

        def dep(d):
            if d is None or d is op:
                return
            if d.slot is None and slot is None and d.eng == eng:
                if eng == "pe":
                    return
                if eng in ("act", "dve") and d.sz >= 256:
                    return
            deps[id(d)] = d

        for r in reads:
            for d in r.ws.values():
                dep(d)
        for w in writes:
            for d in w.ws.values():
                dep(d)
            for d in w.rs.values():
                dep(d)
        if slot is not None:
            dep(slot.last)
            slot.last = op
            slot.n += 1
            op.cnt = slot.n
        for d in deps.values():
            d.signal = True
        op.deps = list(deps.values())
        for w in writes:
            w.ws = {op.stream: op}
            w.rs = {}
        for r in reads:
            r.rs[op.stream] = op
        self.ops[eng].append(op)
        return op

    def emit(self):
        nc, gs = self.nc, self.gs
        for e in self.ENGS:
            cops = [o for o in self.ops[e] if o.slot is None]
            if cops:
                cops[-1].signal = True
        final = {}
        for e in self.ENGS:
            c = gs.ecnt[e]
            for op in self.ops[e]:
                if op.slot is None and op.signal:
                    c += 1
                    op.cnt = c
            final[e] = c if c > gs.ecnt[e] else None
            gs.ecnt[e] = c
        used = [s for s in self.slots if s.n > 0]
        for kind in ("hw", "sw"):
            uk = [s for s in used if s.kind == kind]
            order = sorted([i for i in range(len(gs.dsem)) if gs.dkind[i] == kind], key=lambda i: gs.dval[i])
            assert len(uk) <= len(order), f"too many dma slots {len(uk)}"
            for s, i in zip(sorted(uk, key=lambda s: -s.n), order):
                s.pool = i
                s.base = gs.dval[i]
                gs.dval[i] += 16 * s.n
                assert gs.dval[i] < 60000, "dma semaphore overflow"

        nE = len(gs.esem["pe"])

        def target(d):
            if d.slot is not None:
                return gs.dsem[d.slot.pool], d.slot.base + 16 * d.cnt
            i = d.cnt - 1
            ep = i // SEM_LIMIT
            assert ep < nE, "engine semaphore overflow"
            return gs.esem[d.eng][ep], (i % SEM_LIMIT) + 1

        with nc.Block() as block:
            def run(e, eng):
                known = {}

                def wait(sem, val):
                    k = id(sem)
                    if known.get(k, 0) >= val:
                        return
                    known[k] = val
                    eng.wait_ge(sem, val)

                for op in self.ops[e]:
                    for d in op.deps:
                        wait(*target(d))
                    ins = op.fn(eng)
                    if op.slot is not None:
                        sem, val = target(op)
                        ins.then_inc(sem, 16)
                    elif op.signal:
                        sem, val = target(op)
                        ins.then_inc(sem, 1)
                for e2 in self.ENGS:
                    if final[e2] is not None:
                        i = final[e2] - 1
                        wait(gs.esem[e2][i // SEM_LIMIT], (i % SEM_LIMIT) + 1)
                for s in used:
                    wait(gs.dsem[s.pool], s.base + 16 * s.n)

            @block.tensor
            def _(eng):
                run("pe", eng)

            @block.scalar
            def _(eng):
                run("act", eng)

            @block.vector
            def _(eng):
                run("dve", eng)

            @block.gpsimd
            def _(eng):
                run("pool", eng)

            @block.sync
            def _(eng):
                run("sp", eng)


class Tl:
    def __init__(self, ph, t, name):
        self.t = t
        self.r = Res(name)
        self.ph = ph
        self.name = name
        self._slot = {}

    def __getitem__(self, k):
        return self.t[k]

    def slot(self, q):
        kind = "hw" if q == "sp" else "sw"
        if kind not in self._slot:
            self._slot[kind] = self.ph.S.slot(self.name + kind, kind)
        return self._slot[kind]


class Phase:
    def __init__(self, K, name):
        self.K = K
        self.nc = K.nc
        self.name = name
        self.es = ExitStack()
        self.S = Sched(K.nc, K.gs)
        self.dr = {}
        self.n = 0

    def __enter__(self):
        self.es.__enter__()
        return self

    def __exit__(self, *a):
        if a[0] is None:
            self.S.emit()
        return self.es.__exit__(*a)

    def sb(self, name, shape, dt):
        self.n += 1
        t = self.es.enter_context(self.nc.sbuf_tensor(f"{self.name}_{name}_{self.n}", list(shape), dt))
        return Tl(self, t, name)

    def ps(self, name, shape, dt=F32):
        self.n += 1
        t = self.es.enter_context(self.nc.psum_tensor(f"{self.name}_{name}_{self.n}", list(shape), dt))
        return Tl(self, t, name)

    def dres(self, key):
        r = self.dr.get(key)
        if r is None:
            r = Res(str(key))
            self.dr[key] = r
        return r

    def op(self, eng, fn, rd=(), wr=(), sz=0):
        rd = [x.r if isinstance(x, Tl) else x for x in rd]
        wr = [x.r if isinstance(x, Tl) else x for x in wr]
        return self.S.add(eng, fn, rd, wr, sz=sz)

    def load(self, q, tile, out_ap, in_ap, rd=(), extra_wr=()):
        rd = [x.r if isinstance(x, Tl) else x for x in rd]
        wr = [tile.r] + [x.r if isinstance(x, Tl) else x for x in extra_wr]
        return self.S.add(q, lambda e: e.dma_start(out=out_ap, in_=in_ap), rd, wr, slot=tile.slot(q))

    def store(self, q, tile, out_ap, in_ap, wr=()):
        wr = [x.r if isinstance(x, Tl) else x for x in wr]
        return self.S.add(q, lambda e: e.dma_start(out=out_ap, in_=in_ap), [tile.r], wr, slot=tile.slot(q))

    def mm(self, out_t, out_ap, lhsT_ap, rhs_ap, start, stop, rd):
        return self.op("pe", lambda e: e.matmul(out_ap, lhsT=lhsT_ap, rhs=rhs_ap, start=start, stop=stop), rd, [out_t])

    def tr(self, out_t, out_ap, in_ap, ident_ap, rd):
        return self.op("pe", lambda e: e.transpose(out_ap, in_ap, ident_ap), rd, [out_t])

    def act(self, out_t, out_ap, in_ap, func, rd, bias=None, scale=None, accum=None, extra_wr=()):
        kw = {}
        if bias is not None:
            kw["bias"] = bias
        if scale is not None:
            kw["scale"] = scale
        if accum is not None:
            kw["accum_out"] = accum
        return self.op("act", lambda e: e.activation(out=out_ap, in_=in_ap, func=func, **kw), rd, [out_t] + list(extra_wr),
                       sz=0 if accum is not None else out_ap.free_size())

    def tt(self, out_t, out_ap, in0, in1, op, rd, eng="dve"):
        return self.op(eng, lambda e: e.tensor_tensor(out=out_ap, in0=in0, in1=in1, op=op), rd, [out_t], sz=out_ap.free_size())

    def stt(self, out_t, out_ap, in0, scalar, in1, op0, op1, rd):
        return self.op("dve", lambda e: e.scalar_tensor_tensor(out=out_ap, in0=in0, scalar=scalar, in1=in1, op0=op0, op1=op1), rd, [out_t], sz=out_ap.free_size())

    def tsm(self, out_t, out_ap, in0, scalar, rd, eng="dve"):
        return self.op(eng, lambda e: e.tensor_scalar_mul(out=out_ap, in0=in0, scalar1=scalar), rd, [out_t], sz=out_ap.free_size())

    def cp(self, eng, out_t, out_ap, in_ap, rd):
        if eng == "act":
            return self.op("act", lambda e: e.copy(out=out_ap, in_=in_ap), rd, [out_t], sz=out_ap.free_size())
        return self.op(eng, lambda e: e.tensor_copy(out=out_ap, in_=in_ap), rd, [out_t], sz=out_ap.free_size())

    def rsqrt(self, out_t, out_ap, in_ap, rd, scale=None, tmp_t=None, tmp_ap=None, power=-0.5):
        tt_, ta = (tmp_t, tmp_ap) if tmp_t is not None else (out_t, out_ap)
        self.act(tt_, ta, in_ap, AF.Ln, rd, bias=EPS, scale=scale)
        return self.act(out_t, out_ap, ta, AF.Exp, [tt_], scale=power)


def bc(ap, shape):
    return ap.to_broadcast(list(shape))


class Kernel:
    def __init__(self):
        nc = bass.Bass("TRN2", target_bir_lowering=False)
        self.nc = nc
        D = nc.dram_tensor

        def inp(name, shape, dt=F32):
            return D(name, list(shape), dt, kind="ExternalInput").ap()

        def outp(name, shape):
            return D(name, list(shape), F32, kind="ExternalOutput").ap()

        def scr(name, shape, dt):
            return D(name, list(shape), dt).ap()

        self.x_tok = inp("x_tok", [T, 1024])
        self.cvec = inp("cvec", [128, 8, 2])
        self.cache_k = inp("cache_k", [4, 256, 256])
        self.cache_v = inp("cache_v", [4, 256, 256])
        self.state = inp("state", [4, 2, 2048, 128])
        self.w_mod = inp("w_mod", [4, 1024, 6144])
        self.b_mod = inp("b_mod", [128, 4, 48])
        self.w_in = inp("w_in", [4, 1024, 8768])
        self.qkn = inp("qkn", [128, 4, 2])
        self.conv_w = inp("conv_w", [128, 4, 24, 3])
        self.conv_b = inp("conv_b", [128, 4, 24])
        self.dt_bias = inp("dt_bias", [4, 64])
        self.a_log = inp("a_log", [4, 64])
        self.d_skip = inp("d_skip", [4, 32])
        self.ssd_norm = inp("ssd_norm", [4, 2048])
        self.w_attn_o = inp("w_attn_o", [4, 1024, 1024])
        self.w_ssd_o = inp("w_ssd_o", [4, 2048, 1024])
        self.w_out = inp("w_out", [4, 1024, 1024])
        self.w_up = inp("w_up", [4, 1024, 5632])
        self.ffn_cw = inp("ffn_cw", [128, 4, 44, 3])
        self.ffn_cb = inp("ffn_cb", [128, 4, 44])
        self.w_down = inp("w_down", [4, 2816, 1024])
        self.rope = inp("rope", [2, 128, 4096])
        self.cmask = inp("cmask", [128, 6, 128])
        self.prot = inp("prot", [128, 128])

        self.y_tok = outp("y_tok", [T, 1024])
        self.nck = outp("nck", [2, 4, 256, 256])
        self.ncv = outp("ncv", [2, 4, 256, 256])
        self.nss = outp("nss", [2, 4, 2, 2048, 128])

        self.xT = scr("xT", [8, 128, T], F32)
        self.hT = scr("hT", [8, 128, T], BF16)
        self.MOD = scr("MOD", [128, 4, 48, 2], F32)
        self.qT = scr("qT", [8, 128, T], BF16)
        self.kT = scr("kT", [2, 128, T], BF16)
        self.Vt = scr("Vt", [T, 256], BF16)
        self.Zs = scr("Zs", [T, 2048], BF16)
        self.DT = scr("DT", [T, 64], F32)
        self.XS = scr("XS", [T, 2048], BF16)
        self.Bt = scr("Bt", [T, 512], BF16)
        self.BT = scr("BT", [4, 128, T], BF16)
        self.CT = scr("CT", [4, 128, T], BF16)
        self.GA = scr("GA", [16, 128, T], BF16)
        self.AT = scr("AT", [8, 128, T], BF16)
        self.Y1 = scr("Y1", [T, 2048], F32)
        self.yT = scr("yT", [16, 128, T], BF16)
        self.actT = scr("actT", [22, 128, T], BF16)

        with ExitStack() as es:
            self.gs = GlobalSems(nc, es)
            self.build()

    def build(self):
        self.ph_init()
        self.ph_norm(0, 0)
        for l in range(DEPTH):
            self.ph_inproj_tok(l)
            self.ph_inproj_feat(l, 0)
            self.ph_inproj_feat(l, 1)
            self.ph_attn(l)
            self.ph_ssd(l, 0)
            self.ph_ssd(l, 1)
            self.ph_merge(l)
            self.ph_ffn_up(l)
            self.ph_ffn_down(l)
        self.ph_final()

    def consts(self, ph, want):
        c = {}
        if "mask" in want:
            m = ph.sb("cmask", [128, 6, 128], F32)
            ph.load("sp", m, m[:], self.cmask)
            c["mask"] = m
        if "maskb" in want:
            mb = ph.sb("cmaskb", [128, 6, 128], BF16)
            ph.load("pool", mb, mb[:], self.cmask)
            c["maskb"] = mb
        return c

    def ph_init(self):
        with Phase(self, "init") as ph:
            c = self.consts(ph, ["mask"])
            m = c["mask"]
            ident = m[:, 4, :]
            xt = [ph.sb(f"xt{i}", [128, 4, 1024], F32) for i in range(2)]
            st = [ph.sb(f"st{i}", [128, 8, 512], F32) for i in range(2)]
            pt = [ph.ps(f"pt{i}", [128, 512]) for i in range(4)]
            xv = self.x_tok.rearrange("(t j p) f -> t p j f", j=4, p=128)
            for t in range(NT):
                x_ = xt[t % 2]
                s_ = st[t % 2]
                ph.load("sp", x_, x_[:], xv[t])
                for cch in range(8):
                    p_ = pt[cch % 4]
                    for j in range(4):
                        ph.tr(p_, p_[:, j * 128:(j + 1) * 128], x_[:, j, cch * 128:(cch + 1) * 128], ident, [x_, m])
                    ph.cp("act" if cch % 2 else "dve", s_, s_[:, cch, :], p_[:], [p_])
                ph.store("pool", s_, self.xT[:, :, t * 512:(t + 1) * 512].rearrange("c p t -> p c t"), s_[:])
            cv = ph.sb("cv", [128, 8, 2], F32)
            ph.load("sp", cv, cv[:], self.cvec)
            ph.act(cv, cv[:], cv[:], AF.Silu, [cv])
            bm = ph.sb("bm", [128, 4, 48], F32)
            ph.load("sp", bm, bm[:], self.b_mod)
            modt = ph.sb("modt", [128, 4, 48, 2], F32)
            wm = [ph.sb(f"wm{i}", [128, 8, 1024], F32) for i in range(2)]
            pm = ph.ps("pm", [128, 96])
            k = 0
            for l in range(DEPTH):
                for s in range(6):
                    w_ = wm[k % 2]
                    k += 1
                    ph.load("sp", w_, w_[:], self.w_mod[l, :, s * 1024:(s + 1) * 1024].rearrange("(c p) n -> p c n", p=128))
                    for j in range(8):
                        col = (s * 8 + j) * 2
                        for kc in range(8):
                            ph.mm(pm, pm[:, col:col + 2], w_[:, kc, j * 128:(j + 1) * 128], cv[:, kc, :], kc == 0, kc == 7, [w_, cv])
                ph.tt(modt, modt[:, l, :, :], pm[:].rearrange("p (c r) -> p c r", r=2),
                      bc(bm[:, l, :].unsqueeze(2), [128, 48, 2]), ALU.add, [pm, bm])
                for c0 in (8, 32):
                    ph.op("dve", lambda e, l=l, c0=c0: e.tensor_scalar_add(out=modt[:, l, c0:c0 + 8, :], in0=modt[:, l, c0:c0 + 8, :], scalar1=1.0), [modt], [modt])
            ph.store("pool", modt, self.MOD, modt[:])

    def ph_norm(self, l, k):
        with Phase(self, f"norm{l}_{k}") as ph:
            mod = ph.sb("mod", [128, 4, 48, 2], F32)
            ph.load("sp", mod, mod[:], self.MOD)
            ones = ph.sb("ones", [128, 128], BF16)
            ph.op("dve", lambda e: e.memset(ones[:], 1.0 / 1024.0), [], [ones])
            xs = [ph.sb(f"x{i}", [128, 8, 512], F32) for i in range(2)]
            sq = ph.sb("sq", [128, 8, 512], BF16)
            hs = [ph.sb(f"h{i}", [128, 8, 512], BF16) for i in range(2)]
            r1 = ph.sb("r1", [128, 512], F32)
            rstd = ph.sb("rstd", [128, 512], F32)
            pms = [ph.ps(f"pms{i}", [128, 512]) for i in range(2)]
            for t in range(NT):
                cond = 0 if t < 8 else 1
                x_, h_, pm = xs[t % 2], hs[t % 2], pms[t % 2]
                ph.load("sp", x_, x_[:], self.xT[:, :, t * 512:(t + 1) * 512].rearrange("c p t -> p c t"))
                ph.act(sq, sq[:], x_[:], AF.Square, [x_])
                for cch in range(8):
                    ph.mm(pm, pm[:], ones[:], sq[:, cch, :], cch == 0, cch == 7, [ones, sq])
                ph.rsqrt(rstd, rstd[:], pm[:], [pm], tmp_t=r1, tmp_ap=r1[:])
                ph.tt(x_, x_[:], x_[:], bc(rstd[:].unsqueeze(1), [128, 8, 512]), ALU.mult, [x_, rstd])
                for cch in range(8):
                    ph.act(h_, h_[:, cch, :], x_[:, cch, :], AF.Identity, [x_, mod],
                           scale=mod[:, l, 24 * k + 8 + cch, cond:cond + 1], bias=mod[:, l, 24 * k + cch, cond:cond + 1])
                ph.store("pool", h_, self.hT[:, :, t * 512:(t + 1) * 512].rearrange("c p t -> p c t"), h_[:])

    def norm_bufs(self, ph):
        ones = ph.sb("nones", [128, 128], BF16)
        ph.op("dve", lambda e: e.memset(ones[:], 1.0 / 1024.0), [], [ones])
        return dict(ones=ones, sq=ph.sb("nsq", [128, 8, 512], BF16), tmp=ph.sb("ntmp", [128, 8, 512], F32),
                    hs=[ph.sb(f"nh{i}", [128, 8, 512], BF16) for i in range(2)], r1=ph.sb("nr1", [128, 512], F32),
                    rstd=ph.sb("nrstd", [128, 512], F32), pm=ph.ps("npm", [128, 512]))

    def emit_norm(self, ph, nb, mod, x_, l, k, t):
        cond = 0 if t < 8 else 1
        sq, tmp, h_, pm, rstd = nb["sq"], nb["tmp"], nb["hs"][t % 2], nb["pm"], nb["rstd"]
        ph.act(sq, sq[:], x_[:], AF.Square, [x_])
        for cch in range(8):
            ph.mm(pm, pm[:], nb["ones"][:], sq[:, cch, :], cch == 0, cch == 7, [nb["ones"], sq])
        ph.rsqrt(rstd, rstd[:], pm[:], [pm], tmp_t=nb["r1"], tmp_ap=nb["r1"][:])
        ph.tt(tmp, tmp[:], x_[:], bc(rstd[:].unsqueeze(1), [128, 8, 512]), ALU.mult, [x_, rstd])
        for cch in range(8):
            ph.act(h_, h_[:, cch, :], tmp[:, cch, :], AF.Identity, [tmp, mod],
                   scale=mod[:, l, 24 * k + 8 + cch, cond:cond + 1], bias=mod[:, l, 24 * k + cch, cond:cond + 1])
        ph.store("pool", h_, self.hT[:, :, t * 512:(t + 1) * 512].rearrange("c p t -> p c t"), h_[:])

    def load_w(self, ph, tile, dst_ap_fn, src, c0, ncols, piece=512):
        o = 0
        while o < ncols:
            n = min(piece, ncols - o)
            ph.load("pool", tile, dst_ap_fn(o, n), src[:, c0 + o:c0 + o + n].rearrange("(c p) n -> p c n", p=128))
            o += n

    def ph_inproj_tok(self, l):
        with Phase(self, f"ipt{l}") as ph:
            wt = ph.sb("wt", [128, 8, 2368], BF16)
            self.load_w(ph, wt, lambda o, n: wt[:, :, o:o + n], self.w_in[l], 1280, 2304)
            self.load_w(ph, wt, lambda o, n: wt[:, :, 2304 + o:2304 + o + n], self.w_in[l], 6656, 64)
            dtb = ph.sb("dtb", [128, 64], F32)
            ph.load("sp", dtb, dtb[:], self.dt_bias[l:l + 1, :].partition_broadcast(128))
            hts = [ph.sb(f"ht{i}", [128, 8, 512], BF16) for i in range(2)]
            pz = [ph.ps(f"pz{i}", [128, 512]) for i in range(4)]
            pv = ph.ps("pv", [128, 256])
            pd = ph.ps("pd", [128, 64])
            vb = [ph.sb(f"vb{i}", [128, 256], BF16) for i in range(2)]
            v32 = [ph.sb(f"v32{i}", [128, 256], F32) for i in range(2)]
            zs = [ph.sb(f"zs{i}", [128, 2048], BF16) for i in range(2)]
            d1 = [ph.sb(f"d1{i}", [128, 64], F32) for i in range(2)]
            for t in range(NT):
                ht = hts[t % 2]
                ph.load("sp", ht, ht[:], self.hT[:, :, t * 512:(t + 1) * 512].rearrange("c p t -> p c t"))
                for j in range(4):
                    blk = 4 * t + j
                    r0 = blk * 128
                    i2 = blk % 2
                    hsl = lambda kc: ht[:, kc, j * 128:(j + 1) * 128]
                    for kc in range(8):
                        ph.mm(pv, pv[:], hsl(kc), wt[:, kc, 0:256], kc == 0, kc == 7, [ht, wt])
                    for g in range(4):
                        for kc in range(8):
                            ph.mm(pz[g], pz[g][:], hsl(kc), wt[:, kc, 256 + g * 512:256 + (g + 1) * 512], kc == 0, kc == 7, [ht, wt])
                    for kc in range(8):
                        ph.mm(pd, pd[:], hsl(kc), wt[:, kc, 2304:2368], kc == 0, kc == 7, [ht, wt])
                    ph.cp("act", vb[i2], vb[i2][:], pv[:], [pv])
                    ph.store("pool", vb[i2], self.Vt[r0:r0 + 128, :], vb[i2][:])
                    if blk >= 32:
                        s = (blk - 32) // 2
                        tr0 = ((blk - 32) % 2) * 128
                        ph.cp("dve", v32[i2], v32[i2][:], pv[:], [pv])
                        ph.store("pool", v32[i2], self.ncv[s, l, tr0:tr0 + 128, :], v32[i2][:])
                    for g in range(4):
                        ph.act(zs[i2], zs[i2][:, g * 512:(g + 1) * 512], pz[g][:], AF.Silu, [pz[g]])
                    ph.store("pool", zs[i2], self.Zs[r0:r0 + 128, :], zs[i2][:])
                    ph.tt(d1[i2], d1[i2][:], pd[:], dtb[:], ALU.add, [pd, dtb])
                    ph.act(d1[i2], d1[i2][:], d1[i2][:], AF.Exp, [d1[i2]])
                    ph.act(d1[i2], d1[i2][:], d1[i2][:], AF.Ln, [d1[i2]], bias=1.0)
                    ph.store("pool", d1[i2], self.DT[r0:r0 + 128, :], d1[i2][:])

    def conv_chunk(self, ph, raw, acc, cw, cb_ap, w_aps, a0=0, n=RAWN - 2):
        ph.tsm(acc, acc[:, 0:n], raw[:, a0:a0 + n], w_aps[0], [raw, cw])
        ph.stt(acc, acc[:, 0:n], raw[:, a0 + 1:a0 + n + 1], w_aps[1], acc[:, 0:n], ALU.mult, ALU.add, [raw, cw, acc])
        ph.stt(acc, acc[:, 0:n], raw[:, a0 + 2:a0 + n + 2], w_aps[2], acc[:, 0:n], ALU.mult, ALU.add, [raw, cw, acc])

    def evac_raw(self, ph, raw, p_, t, eng="act"):
        if t < 8:
            ph.cp(eng, raw, raw[:, 1 + 512 * t:1 + 512 * (t + 1)], p_[:], [p_])
        else:
            ph.cp(eng, raw, raw[:, 4098:4354], p_[:, 0:256], [p_])
            ph.cp(eng, raw, raw[:, 4355:4611], p_[:, 256:512], [p_])

    def ph_inproj_feat(self, l, part):
        with Phase(self, f"ipf{l}_{part}") as ph:
            c = self.consts(ph, ["mask", "maskb"])
            m, mb = c["mask"], c["maskb"]
            identf = m[:, 4, :]
            identb = mb[:, 4, :]
            ones128 = ph.sb("ones128", [128, 128], BF16)
            ph.op("dve", lambda e: e.memset(ones128[:], 1.0 / 128.0), [], [ones128])
            prot = ph.sb("prot", [128, 128], BF16)
            ph.load("pool", prot, prot[:], self.prot)
            qkn = ph.sb("qkn", [128, 4, 2], F32)
            ph.load("sp", qkn, qkn[:], self.qkn)
            cw = ph.sb("cw", [128, 24, 3], F32)
            ph.load("sp", cw, cw[:], self.conv_w[:, l, :, :])
            cb = ph.sb("cb", [128, 24], F32)
            ph.load("sp", cb, cb[:], self.conv_b[:, l, :])
            ws = [ph.sb(f"ws{i}", [128, 8, 512], BF16) for i in range(2)]
            hres = [ph.sb(f"hr{i}", [128, 8, 512], BF16) for i in range(NT)]
            for t in range(NT):
                ph.load("sp", hres[t], hres[t][:], self.hT[:, :, t * 512:(t + 1) * 512].rearrange("c p t -> p c t"))
            pc = [ph.ps(f"pc{i}", [128, 512]) for i in range(4)]
            if part == 0:
                pmn = ph.ps("pmn", [128, 512])
                prr = [ph.ps(f"prr{i}", [128, 512]) for i in range(2)]
            else:
                ptbs = [ph.ps(f"ptb{i}", [128, 1024], BF16) for i in range(2)]
            HN0 = 2304
            if part == 1:
                raws = [ph.sb(f"raw{i}", [128, RAWN], F32) for i in range(4)]
                for r_ in raws:
                    ph.op("pool", lambda e, r_=r_: e.memset(r_[:], 0.0), [], [r_])
                acc = ph.sb("acc", [128, RAWN - 2 - HN0], F32)
                cvos = [ph.sb(f"cvo{i}", [128, RAWN], BF16) for i in range(2)]
                stgs = [ph.sb(f"stg{i}", [128, 12, 128], BF16) for i in range(2)]
            else:
                cs = [ph.sb(f"cos{i}", [128, 2, 512], F32) for i in range(2)]
                sqs = [ph.sb(f"sq{i}", [128, 512], BF16) for i in range(2)]
                rstds = [ph.sb(f"rstd{i}", [128, 512], F32) for i in range(2)]
                qns = [ph.sb(f"qn{i}", [128, 512], F32) for i in range(2)]
                qss = [ph.sb(f"qs{i}", [128, 512], BF16) for i in range(2)]
                t1s = [ph.sb(f"t1{i}", [128, 512], F32) for i in range(2)]
                t2s = [ph.sb(f"t2{i}", [128, 512], F32) for i in range(2)]
                ostg = [ph.sb(f"ostg{i}", [128, 4, 512], BF16) for i in range(2)]
                kf = ph.sb("kf", [128, 512], F32)
                kout = ph.sb("kout", [128, 4, 128], F32)

            slabs = []
            if part == 0:
                slabs += [("q", 0, 4, 0), ("q", 512, 4, 4), ("k", 1024, 2, 0)]
                slabs += [("g", 6720 + 512 * i, 4, 4 * i) for i in range(4)]
            else:
                slabs += [("xbc", 3584 + 256 * i, 2, 2 * i) for i in range(12)]
            state = dict(h=0, k=0, x=0, st=0, tb=0)
            pend = []
            xpar = {}

            def load_slab(i):
                kind, c0, nch, cb0 = slabs[i]
                w_ = ws[i % 2]
                self.load_w(ph, w_, lambda o, n: w_[:, :, o:o + n], self.w_in[l], c0, nch * 128)

            def qk_slab(si):
                kind, c0, nch, cb0 = slabs[si]
                w_ = ws[si % 2]
                nidx = 0 if kind == "q" else 1
                nsc = qkn[:, l, nidx:nidx + 1]
                units = [(t, cj) for t in range(NT) for cj in range(nch)]
                n = len(units)
                hcur = {}

                def A(u):
                    t, cj = units[u]
                    if cj == 0:
                        hcur[t] = hres[t]
                        if t < 8:
                            cs_ = cs[t % 2]
                            ph.load("sp", cs_, cs_[:], self.rope[:, :, t * 512:(t + 1) * 512].rearrange("r p t -> p r t"))
                    ht = hcur[t]
                    p_ = pc[u % 4]
                    for kc in range(8):
                        ph.mm(p_, p_[:], w_[:, kc, cj * 128:(cj + 1) * 128], ht[:, kc, :], kc == 0, kc == 7, [w_, ht])
                    sq = sqs[u % 2]
                    ph.act(sq, sq[:], p_[:], AF.Square, [p_])

                def B(u):
                    t, cj = units[u]
                    p_ = pc[u % 4]
                    kk = u % 2
                    sq, rstd, qn, qs = sqs[kk], rstds[kk], qns[kk], qss[kk]
                    os_ = ostg[t % 2]
                    ph.mm(pmn, pmn[:], ones128[:], sq[:], True, True, [ones128, sq])
                    ph.rsqrt(rstd, rstd[:], pmn[:], [pmn])
                    ph.tt(qn, qn[:], p_[:], rstd[:], ALU.mult, [p_, rstd])
                    if t == 8:
                        ph.act(os_, os_[:, cj, :], qn[:], AF.Identity, [qn, qkn], scale=nsc)
                        if kind == "k":
                            ptf = prr[kk]
                            ph.tsm(kf, kf[:], qn[:], nsc, [qn, qkn])
                            for j in range(4):
                                ph.tr(ptf, ptf[:, j * 128:(j + 1) * 128], kf[:, j * 128:(j + 1) * 128], identf, [kf, m])
                            ph.cp("dve", kout, kout[:].rearrange("p j d -> p (j d)"), ptf[:], [ptf])
                            for s_ in range(2):
                                ph.store("pool", kout,
                                         self.nck[s_, l].rearrange("(j p) c -> p j c", p=128)[:, :, cj * 128:(cj + 1) * 128],
                                         kout[:, 2 * s_:2 * s_ + 2, :])
                    else:
                        ph.act(qs, qs[:], qn[:], AF.Identity, [qn, qkn], scale=nsc)

                def C(u):
                    t, cj = units[u]
                    kk = u % 2
                    qs, t1, t2, pr = qss[kk], t1s[kk], t2s[kk], prr[kk]
                    os_ = ostg[t % 2]
                    if t < 8:
                        cs_ = cs[t % 2]
                        ph.mm(pr, pr[:], prot[:], qs[:], True, True, [prot, qs])
                        ph.tt(t1, t1[:], qs[:], cs_[:, 0, :], ALU.mult, [qs, cs_])
                        ph.tt(t2, t2[:], pr[:], cs_[:, 1, :], ALU.mult, [pr, cs_])
                        ph.tt(os_, os_[:, cj, :], t1[:], t2[:], ALU.add, [t1, t2], eng="pool")
                    if cj == nch - 1:
                        tsl = slice(t * 512, (t + 1) * 512)
                        if kind == "q":
                            ph.store("pool", os_, self.qT[cb0:cb0 + 4, :, tsl].rearrange("c p t -> p c t"), os_[:])
                        else:
                            ph.store("pool", os_, self.kT[:, :, tsl].rearrange("c p t -> p c t"), os_[:, 0:2, :])

                for u in range(n + 2):
                    if u < n:
                        A(u)
                    if 0 <= u - 1 < n:
                        B(u - 1)
                    if 0 <= u - 2 < n:
                        C(u - 2)

            def mm_slab(si):
                kind, c0, nch, cb0 = slabs[si]
                if kind in ("q", "k"):
                    return qk_slab(si)
                w_ = ws[si % 2]
                if kind == "xbc":
                    xpar[si] = state["x"] % 2
                    state["x"] += 1
                for t in range(NT):
                    ht = hres[t]
                    os_ = ostg[t % 2] if kind != "xbc" else None
                    for cj in range(nch):
                        p_ = pc[cj] if kind != "xbc" else pc[(2 * t + cj) % 4]
                        for kc in range(8):
                            ph.mm(p_, p_[:], w_[:, kc, cj * 128:(cj + 1) * 128], ht[:, kc, :], kc == 0, kc == 7, [w_, ht])
                        if kind == "xbc":
                            self.evac_raw(ph, raws[2 * xpar[si] + cj], p_, t)
                        else:
                            ph.act(os_, os_[:, cj, :], p_[:], AF.Sigmoid, [p_])
                    tsl = slice(t * 512, (t + 1) * 512)
                    if kind == "g":
                        ph.store("pool", os_, self.GA[cb0:cb0 + 4, :, tsl].rearrange("c p t -> p c t"), os_[:])
                    for _ in range(2):
                        if pend:
                            p = pend.pop(0)
                            if p is not None:
                                p()

            def post_slab(si):
                kind, c0, nch, cb0 = slabs[si]
                if kind != "xbc":
                    return
                convs, trs = [], []
                for cj in range(nch):
                    ch = cb0 + cj
                    raw = raws[2 * xpar[si] + cj]
                    cvo = cvos[cj % 2]
                    cp_, tp_ = [], []
                    for (a0, hn) in ((0, HN0), (HN0, RAWN - 2 - HN0)):
                        def conv_piece(raw=raw, ch=ch, a0=a0, hn=hn, cvo=cvo):
                            self.conv_chunk(ph, raw, acc, cw, None, [cw[:, ch, i:i + 1] for i in range(3)], a0=a0, n=hn)
                            ph.act(cvo, cvo[:, a0:a0 + hn], acc[:, 0:hn], AF.Silu, [acc, cb], bias=cb[:, ch:ch + 1])
                        cp_.append(conv_piece)
                    if ch >= 16:
                        def ft_piece(ch=ch, cvo=cvo):
                            dst = self.BT if ch < 20 else self.CT
                            gi = (ch - 16) % 4
                            for (tok0, a0, n) in SEGS:
                                ph.store("pool", cvo, dst[gi, :, tok0:tok0 + n], cvo[:, a0:a0 + n])
                        tp_.append(ft_piece)
                    if ch < 20:
                        if ch < 16:
                            dv = self.XS.rearrange("(b p) c -> p b c", p=128)[:, :, ch * 128:(ch + 1) * 128]
                        else:
                            dv = self.Bt.rearrange("(b p) c -> p b c", p=128)[:, :, (ch - 16) * 128:(ch - 15) * 128]
                        for b0 in range(0, NB, 12):
                            def tr_piece(b0=b0, dv=dv, cvo=cvo):
                                stg = stgs[state["st"] % 2]
                                state["st"] += 1
                                for b1 in range(0, 12, 6):
                                    ptb = ptbs[state["tb"] % 2]
                                    state["tb"] += 1
                                    for bi in range(6):
                                        blk = b0 + b1 + bi
                                        tok = blk * 128
                                        for (tok0, a0, n) in SEGS:
                                            if tok0 <= tok < tok0 + n:
                                                a = a0 + tok - tok0
                                        ph.tr(ptb, ptb[:, bi * 128:(bi + 1) * 128], cvo[:, a:a + 128], identb, [cvo, mb])
                                    ph.cp("act", stg, stg[:, b1:b1 + 6, :].rearrange("p b c -> p (b c)"), ptb[:, 0:6 * 128], [ptb])
                                ph.store("pool", stg, dv[:, b0:b0 + 12, :], stg[:])
                            tp_.append(tr_piece)
                    convs.append(cp_)
                    trs.append(tp_)
                left = state.get("left", [])
                left = (left + [None] * 4)[:max(4, len(left))]
                cur = convs[0] + left + convs[1] + trs[0]
                assert len(cur) <= 18, len(cur)
                cur = cur + [None] * (18 - len(cur))
                pend.extend(cur)
                state["left"] = trs[1]

            n = len(slabs)
            load_slab(0)
            if n > 1:
                load_slab(1)
            mm_slab(0)
            for si in range(n):
                if si + 2 < n:
                    load_slab(si + 2)
                post_slab(si)
                if si + 1 < n:
                    mm_slab(si + 1)
            pend.extend(state.get("left", []))
            while pend:
                p = pend.pop(0)
                if p is not None:
                    p()

    def ph_attn(self, l):
        with Phase(self, f"attn{l}") as ph:
            c = self.consts(ph, ["mask"])
            m = c["mask"]
            identf = m[:, 4, :]
            onesf = m[:, 5, :]
            kall = ph.sb("kall", [128, 2, 4352], BF16)
            vall = ph.sb("vall", [128, 34, 256], BF16)
            ck = ph.sb("ck", [128, 2, 256], F32)
            NQ = 3
            qb_ = [ph.sb(f"qb{i}", [128, 4, 128], BF16) for i in range(NQ)]
            NE = 5
            E = [ph.sb(f"E{i}", [128, 512], BF16) for i in range(NE)]
            NS = 4
            pS = [ph.ps(f"pS{i}", [128, 512]) for i in range(NS)]
            pO = [ph.ps(f"pO{i}", [128, 512]) for i in range(2)]
            pD = [ph.ps(f"pD{i}", [128, 512]) for i in range(2)]
            pT = pS[0]
            accs = [ph.sb(f"acc{i}", [128, 512], F32) for i in range(2)]
            accp = [ph.sb(f"accp{i}", [128, 512], F32) for i in range(2)]
            rec = ph.sb("rec", [128, 512], F32)
            ao = [ph.sb(f"ao{i}", [128, 4, 128], BF16) for i in range(2)]
            scale = 128.0 ** -0.5
            LA = 3
            blocks = []
            for (tok0, ntok, kind) in SEQS:
                nk = 34 if kind == 0 else 2
                for g in range(2):
                    for qb in range(ntok // 128):
                        blocks.append(dict(tok0=tok0, kind=kind, nk=nk, g=g, q0=tok0 + qb * 128, idx=len(blocks)))
            items = [(b, kt) for b in blocks for kt in range(b["nk"])]
            loaded_seq = [None]

            def load_seq(kind, tok0):
                if loaded_seq[0] == kind:
                    return
                loaded_seq[0] = kind
                if kind == 0:
                    ph.load("sp", ck, ck[:], self.cache_k[l].rearrange("(j p) c -> p j c", p=128))
                    for g in range(2):
                        for j in range(2):
                            ph.tr(pT, pT[:, (2 * g + j) * 128:(2 * g + j + 1) * 128], ck[:, j, g * 128:(g + 1) * 128], identf, [ck, m])
                        ph.cp("dve", kall, kall[:, g, 0:256], pT[:, 2 * g * 128:(2 * g + 2) * 128], [pT])
                    ph.load("sp", kall, kall[:, :, 256:4352], self.kT[:, :, 0:4096].rearrange("g p t -> p g t"))
                    ph.load("pool", vall, vall[:, 0:2, :], self.cache_v[l].rearrange("(j p) c -> p j c", p=128))
                    for j0 in range(0, 32, 8):
                        ph.load("sp", vall, vall[:, 2 + j0:2 + j0 + 8, :],
                                self.Vt[j0 * 128:(j0 + 8) * 128, :].rearrange("(j p) c -> p j c", p=128))
                else:
                    ph.load("sp", kall, kall[:, :, 0:256], self.kT[:, :, tok0:tok0 + 256].rearrange("g p t -> p g t"))
                    ph.load("sp", vall, vall[:, 0:2, :], self.Vt[tok0:tok0 + 256, :].rearrange("(j p) c -> p j c", p=128))

            qloaded = set()

            def load_q(b):
                if b["idx"] in qloaded:
                    return
                qloaded.add(b["idx"])
                q_ = qb_[b["idx"] % NQ]
                g, q0 = b["g"], b["q0"]
                ph.load("sp", q_, q_[:], self.qT[4 * g:4 * g + 4, :, q0:q0 + 128].rearrange("h p t -> p h t"))

            def issue_S(i):
                b, kt = items[i]
                load_seq(b["kind"], b["tok0"])
                load_q(b)
                q_ = qb_[b["idx"] % NQ]
                ps_ = pS[i % NS]
                ph.mm(ps_, ps_[:], kall[:, b["g"], kt * 128:(kt + 1) * 128], q_[:].rearrange("p h t -> p (h t)"), True, True, [kall, q_])

            def finalize(b):
                bi = b["idx"]
                po, pd, a_, acc, acp = pO[bi % 2], pD[bi % 2], ao[bi % 2], accs[bi % 2], accp[bi % 2]
                g, q0 = b["g"], b["q0"]
                ph.mm(pd, pd[:], onesf, acc[:], True, True, [m, acc])
                ph.rsqrt(rec, rec[:], pd[:], [pd], power=-1.0)
                ph.tt(a_, a_[:].rearrange("p h t -> p (h t)"), po[:], rec[:], ALU.mult, [po, rec])
                ph.store("pool", a_, self.AT[4 * g:4 * g + 4, :, q0:q0 + 128].rearrange("h p t -> p h t"), a_[:])

            pending = []
            gi = 0
            for (stok0, sntok, skind) in SEQS:
                sidx = [i for i, (b, kt) in enumerate(items) if b["kind"] == skind]
                lo, hi = sidx[0], sidx[-1] + 1
                for i in range(lo, min(lo + LA, hi)):
                    issue_S(i)
                for i in range(lo, hi):
                    b, kt = items[i]
                    if i + LA < hi:
                        issue_S(i + LA)
                    bi = b["idx"]
                    ps_, e_ = pS[i % NS], E[i % NE]
                    po, acc = pO[bi % 2], accs[bi % 2]
                    nk, g = b["nk"], b["g"]
                    ph.act(e_, e_[:], ps_[:], AF.Exp, [ps_], scale=scale)
                    ph.mm(po, po[:], vall[:, kt, g * 128:(g + 1) * 128], e_[:], kt == 0, kt == nk - 1, [vall, e_])
                    if False:
                        a2, eng2 = accp[bi % 2], "pool"
                    else:
                        a2, eng2 = acc, "dve"
                    if kt == 0:
                        ph.cp(eng2, a2, a2[:], e_[:], [e_])
                    else:
                        ph.tt(a2, a2[:], a2[:], e_[:], ALU.add, [a2, e_], eng=eng2)
                    if kt == nk - 1:
                        pending.append((i + 2, b))
                    while pending and pending[0][0] <= i:
                        finalize(pending.pop(0)[1])
                while pending:
                    finalize(pending.pop(0)[1])

    def ph_ssd(self, l, d):
        with Phase(self, f"ssd{l}_{d}") as ph:
            c = self.consts(ph, ["mask"])
            m = c["mask"]
            LE, GE, GT, LT, IDN, ONES = [m[:, i, :] for i in range(6)]
            negA = ph.sb("negA", [128, 64], F32)
            ph.load("sp", negA, negA[:], self.a_log[l:l + 1, :].partition_broadcast(128))
            ph.act(negA, negA[:], negA[:], AF.Exp, [negA])
            ph.op("dve", lambda e: e.tensor_scalar_mul(out=negA[:], in0=negA[:], scalar1=-1.0), [negA], [negA])
            Hs = [ph.sb(f"H{i}", [128, 512], F32) for i in range(4)]
            Hbs = [ph.sb(f"Hb{i}", [128, 512], BF16) for i in range(4)]
            st = ph.sb("st", [128, 16, 128], F32)
            xs = [ph.sb(f"xs{i}", [128, 2048], BF16) for i in range(2)]
            bt = [ph.sb(f"bt{i}", [128, 512], BF16) for i in range(2)]
            CTt = [ph.sb(f"CTt{i}", [128, 4, 128], BF16) for i in range(2)]
            dt = [ph.sb(f"dt{i}", [128, 64], F32) for i in range(2)]
            adts = [ph.sb(f"adt{i}", [128, 64], F32) for i in range(2)]
            css = [ph.sb(f"cs{i}", [128, 192], F32) for i in range(2)]
            for cs in css:
                ph.op("dve", lambda e, cs=cs: e.memset(cs[:], 0.0), [], [cs])
            ecs = [ph.sb(f"ec{i}", [128, 192], F32) for i in range(2)]
            dtes = [ph.sb(f"dte{i}", [128, 32], F32) for i in range(2)]
            xws = [ph.sb(f"xw{i}", [128, 2048], BF16) for i in range(2)]
            pA = ph.ps("pA", [128, 512])
            pc = Tl(ph, pA.t, "pc")
            pYos = [ph.ps(f"pYo{i}", [128, 512]) for i in range(1 if d == 0 else 2)]
            pHs = [ph.ps(f"pH{i}", [128, 512]) for i in range(1 if d == 0 else 2)]
            tmps = [ph.sb(f"tmp{i}", [128, 512], F32) for i in range(2)]
            pS = [ph.ps(f"pS{i}", [128, 512]) for i in range(3 if d == 0 else 1)]
            pT = pS[0]
            if d == 0:
                dsk = ph.sb("dsk", [128, 32], F32)
                ph.load("sp", dsk, dsk[:], self.d_skip[l:l + 1, :].partition_broadcast(128))
                BTt = [ph.sb(f"BTt{i}", [128, 4, 128], BF16) for i in range(2)]
                xdtFs = [ph.sb(f"xdtF{i}", [128, 2048], BF16) for i in range(2)]
                xdtBs = [ph.sb(f"xdtB{i}", [128, 2048], BF16) for i in range(2)]
                pGs = [Tl(ph, pA.t, f"pG{i}") for i in range(2)]
                pYs = [ph.ps(f"pY{i}", [128, 512]) for i in range(2)]
                Gms = [ph.sb(f"Gm{i}", [128, 2, 128], BF16) for i in range(2)]
                Lrs = [ph.sb(f"Lr{i}", [128, 2, 8, 128], F32) for i in range(2)]
                Em = [ph.sb(f"Em{i}", [128, 4, 128], BF16) for i in range(3)]
                Wms = [ph.sb(f"Wm{i}", [128, 2, 8, 128], BF16) for i in range(2)]
                tmp2s = [ph.sb(f"tmp2{i}", [128, 512], F32) for i in range(2)]
                y1 = [ph.sb(f"y1{i}", [128, 2048], F32) for i in range(2)]
            else:
                y1 = [ph.sb(f"y1{i}", [128, 2048], F32) for i in range(2)]
                zs = [ph.sb(f"zs{i}", [128, 2048], BF16) for i in range(2)]
                nw = ph.sb("nw", [128, 2048], F32)
                ph.load("sp", nw, nw[:], self.ssd_norm[l:l + 1, :].partition_broadcast(128))
                mb = ph.sb("identb", [128, 128], BF16)
                ph.op("dve", lambda e: e.tensor_copy(out=mb[:], in_=IDN), [m], [mb])
                junk = ph.sb("junk", [128, 2048], BF16)
                sss = [ph.sb(f"ss{i}", [128, 2], F32) for i in range(2)]
                yns = [ph.sb(f"yn{i}", [128, 2048], BF16) for i in range(2)]
                ptb = ph.ps("ptb", [128, 1024], BF16)
                ynT = [ph.sb(f"ynT{i}", [128, 16, 128], BF16) for i in range(2)]
            ci = 0
            gi = 0
            ri = 0
            for (tok0, ntok, kind) in SEQS:
                if kind == 0:
                    ph.load("sp", st, st[:], self.state[l, d].rearrange("(j p) n -> p j n", p=128))
                    for j in range(16):
                        ph.tr(pT, pT[:, (j % 4) * 128:(j % 4 + 1) * 128], st[:, j, :], IDN, [st, m])
                        if j % 4 == 3:
                            ph.cp("dve", Hs[j // 4], Hs[j // 4][:], pT[:], [pT])
                else:
                    for g in range(4):
                        ph.op("dve", lambda e, g=g: e.memset(Hs[g][:], 0.0), [], [Hs[g]])
                for g in range(4):
                    ph.cp("act", Hbs[g], Hbs[g][:], Hs[g][:], [Hs[g]])
                nch = ntok // 128
                order = list(range(nch)) if d == 0 else list(range(nch - 1, -1, -1))

                def make_chunk(cc):
                    nonlocal ci
                    blk = tok0 // 128 + cc
                    r0 = blk * 128
                    i2 = ci % 2
                    ci += 1
                    xs_, bt_, CT_, dt_ = xs[i2], bt[i2], CTt[i2], dt[i2]
                    adt, cs, ec, dte, xw = adts[i2], css[i2], ecs[i2], dtes[i2], xws[i2]
                    y_ = y1[i2]
                    ecumF, ecumB, cdF, cdB, edsF, edsB = [ec[:, i * 32:(i + 1) * 32] for i in range(6)]
                    xs3 = xs_[:].rearrange("p (h q) -> p h q", q=64)
                    if d == 0:
                        BT_ = BTt[i2]
                        xdtF, xdtB = xdtFs[i2], xdtBs[i2]
                        ecum, cd = ecumF, cdF
                    else:
                        z_ = zs[i2]
                        ecum, cd = ecumB, cdB

                    def P():
                        ph.load("sp", xs_, xs_[:], self.XS[r0:r0 + 128, :])
                        ph.load("sp", bt_, bt_[:], self.Bt[r0:r0 + 128, :])
                        ph.load("sp", CT_, CT_[:], self.CT[:, :, r0:r0 + 128].rearrange("g p t -> p g t"))
                        ph.load("sp", dt_, dt_[:], self.DT[r0:r0 + 128, :])
                        if d == 0:
                            ph.load("sp", BT_, BT_[:], self.BT[:, :, r0:r0 + 128].rearrange("g p t -> p g t"))
                        else:
                            ph.load("sp", y_, y_[:], self.Y1[r0:r0 + 128, :])
                            ph.load("sp", z_, z_[:], self.Zs[r0:r0 + 128, :])
                        ph.tt(adt, adt[:], dt_[:], negA[:], ALU.mult, [dt_, negA])
                        if d == 0:
                            ph.mm(pc, pA[:, 0:32], LE, adt[:, 0:32], True, True, [m, adt])
                        ph.mm(pc, pA[:, 32:64], GE, adt[:, 32:64], True, True, [m, adt])
                        ph.mm(pc, pA[:, 64:128], ONES, adt[:, 0:64], True, True, [m, adt])
                        lo = 0 if d == 0 else 32
                        ph.cp("dve", cs, cs[:, lo:128], pA[:, lo:128], [pc])
                        if d == 0:
                            ph.tt(cs, cs[:, 128:160], cs[:, 64:96], cs[:, 0:32], ALU.subtract, [cs])
                        ph.tt(cs, cs[:, 160:192], cs[:, 96:128], cs[:, 32:64], ALU.subtract, [cs])
                        ph.act(ec, ec[:, lo:192], cs[:, lo:192], AF.Exp, [cs])
                        if d == 0:
                            ph.tt(xdtF, xdtF[:].rearrange("p (h q) -> p h q", q=64), xs3, bc(dt_[:, 0:32].unsqueeze(2), [128, 32, 64]), ALU.mult, [xs_, dt_])
                            ph.tt(xdtB, xdtB[:].rearrange("p (h q) -> p h q", q=64), xs3, bc(dt_[:, 32:64].unsqueeze(2), [128, 32, 64]), ALU.mult, [xs_, dt_], eng="pool")
                            ph.tt(dte, dte[:], dt_[:, 0:32], edsF, ALU.mult, [dt_, ec])
                        else:
                            ph.tt(dte, dte[:], dt_[:, 32:64], edsB, ALU.mult, [dt_, ec])
                        ph.tt(xw, xw[:].rearrange("p (h q) -> p h q", q=64), xs3, bc(dte[:].unsqueeze(2), [128, 32, 64]), ALU.mult, [xs_, dte])

                    bufs = {}

                    def S1(g):
                        nonlocal gi
                        gp = gi % 2
                        gi += 1
                        bufs[g] = gp
                        if d != 0:
                            return
                        pG, Gm, Lr = pGs[gp], Gms[gp], Lrs[gp]
                        pGap = pA[:, 128 * (1 + gp):128 * (2 + gp)]
                        ph.mm(pG, pGap, BT_[:, g, :], CT_[:, g, :], True, True, [BT_, CT_])
                        ph.tt(Gm, Gm[:, 0, :], pGap, LE, ALU.mult, [pG, m])
                        ph.tt(Gm, Gm[:, 1, :], pGap, GE, ALU.mult, [pG, m])
                        for di, (U, R) in enumerate(((GT, LE), (LT, GE))):
                            ph.tt(Lr, Lr[:, di, :, :], bc(R.unsqueeze(1), [128, 8, 128]),
                                  bc(adt[:, 32 * di + 8 * g:32 * di + 8 * g + 8].unsqueeze(2), [128, 8, 128]), ALU.mult, [m, adt],
                                  eng="pool" if di else "dve")

                    def S2(g):
                        nonlocal ri
                        if d != 0:
                            return
                        gp = bufs[g]
                        pY, Gm, Lr, Wm = pYs[gp], Gms[gp], Lrs[gp], Wms[gp]
                        for di, (U, R) in enumerate(((GT, LE), (LT, GE))):
                            for half in range(2):
                                p_, e_ = pS[ri % 3], Em[ri % 3]
                                ri += 1
                                ph.mm(p_, p_[:], U, Lr[:, di, 4 * half:4 * half + 4, :].rearrange("p h t -> p (h t)"), True, True, [m, Lr])
                                ph.act(e_, e_[:].rearrange("p h t -> p (h t)"), p_[:], AF.Exp, [p_])
                                ph.tt(Wm, Wm[:, di, 4 * half:4 * half + 4, :], e_[:], bc(Gm[:, di, :].unsqueeze(1), [128, 4, 128]), ALU.mult, [e_, Gm])
                        for h in range(8):
                            col = (8 * g + h) * 64
                            ph.mm(pY, pY[:, h * 64:(h + 1) * 64], Wm[:, 0, h, :], xdtF[:, col:col + 64], True, False, [Wm, xdtF])
                            ph.mm(pY, pY[:, h * 64:(h + 1) * 64], Wm[:, 1, h, :], xdtB[:, col:col + 64], False, True, [Wm, xdtB])

                    def S3(g):
                        gp = bufs[g]
                        tmp = tmps[gp]
                        pYo, pH = pYos[gp % len(pYos)], pHs[gp % len(pHs)]
                        gsl = slice(g * 512, (g + 1) * 512)
                        hsl = slice(8 * g, 8 * g + 8)
                        H, Hb = Hs[g], Hbs[g]
                        ph.mm(pYo, pYo[:], CT_[:, g, :], Hb[:], True, True, [CT_, Hb])
                        ph.tt(tmp, tmp[:].rearrange("p (h q) -> p h q", q=64), pYo[:].rearrange("p (h q) -> p h q", q=64),
                              bc(ecum[:, hsl].unsqueeze(2), [128, 8, 64]), ALU.mult, [pYo, ec])
                        if d == 0:
                            pY, tmp2 = pYs[gp], tmp2s[gp]
                            ph.tt(y_, y_[:, gsl], tmp[:], pY[:], ALU.add, [tmp, pY])
                            ph.tt(tmp2, tmp2[:].rearrange("p (h q) -> p h q", q=64), xs_[:, gsl].rearrange("p (h q) -> p h q", q=64),
                                  bc(dsk[:, hsl].unsqueeze(2), [128, 8, 64]), ALU.mult, [xs_, dsk], eng="pool")
                            ph.tt(y_, y_[:, gsl], y_[:, gsl], tmp2[:], ALU.add, [y_, tmp2], eng="pool")
                        else:
                            ph.tt(y_, y_[:, gsl], y_[:, gsl], tmp[:], ALU.add, [y_, tmp], eng="pool")
                        ph.mm(pH, pH[:], bt_[:, g * 128:(g + 1) * 128], xw[:, gsl], True, True, [bt_, xw])
                        ph.tt(H, H[:].rearrange("p (h q) -> p h q", q=64), H[:].rearrange("p (h q) -> p h q", q=64),
                              bc(cd[:, hsl].unsqueeze(2), [128, 8, 64]), ALU.mult, [H, ec])
                        ph.tt(H, H[:], H[:], pH[:], ALU.add, [H, pH])
                        ph.cp("act", Hb, Hb[:], H[:], [H])

                    def E():
                        if d == 0:
                            ph.store("pool", y_, self.Y1[r0:r0 + 128, :], y_[:])
                            return
                        ss, yn = sss[i2], yns[i2]
                        ph.tt(y_, y_[:], y_[:], z_[:], ALU.mult, [y_, z_])
                        ph.act(junk, junk[:], y_[:], AF.Square, [y_], accum=ss[:, 0:1], extra_wr=[ss.r])
                        ph.rsqrt(ss, ss[:, 1:2], ss[:, 0:1], [ss], scale=1.0 / 2048.0)
                        ph.stt(yn, yn[:], y_[:], ss[:, 1:2], nw[:], ALU.mult, ALU.mult, [y_, ss, nw])
                        yt_ = ynT[i2]
                        for j0 in (0, 8):
                            for j in range(8):
                                ph.tr(ptb, ptb[:, j * 128:(j + 1) * 128], yn[:, (j0 + j) * 128:(j0 + j + 1) * 128], mb[:], [yn, mb])
                            ph.cp("act", yt_, yt_[:, j0:j0 + 8, :].rearrange("p j t -> p (j t)"), ptb[:], [ptb])
                        ph.store("pool", yt_, self.yT[:, :, r0:r0 + 128].rearrange("j p t -> p j t"), yt_[:])

                    return dict(P=P, S1=S1, S2=S2, S3=S3, E=E)

                chunks = {}

                def get(k):
                    if k not in chunks:
                        chunks[k] = make_chunk(order[k])
                    return chunks[k]

                units = [(k, g) for k in range(nch) for g in range(4)]
                get(0)["P"]()
                get(0)["S1"](0)
                for i, (k, g) in enumerate(units):
                    if i + 1 < len(units):
                        k2, g2 = units[i + 1]
                        if g == 1 and k + 1 < nch:
                            get(k + 1)["P"]()
                        get(k2)["S1"](g2)
                    get(k)["S2"](g)
                    get(k)["S3"](g)
                    if g == 3:
                        get(k)["E"]()
                        if k - 1 in chunks:
                            del chunks[k - 1]
                if kind != 0:
                    s = kind - 1
                    for j in range(16):
                        ph.tr(pT, pT[:, (j % 4) * 128:(j % 4 + 1) * 128], Hs[j // 4][:, (j % 4) * 128:(j % 4 + 1) * 128], IDN, [Hs[j // 4], m])
                        if j % 4 == 3:
                            ph.cp("dve", st, st[:, j - 3:j + 1, :].rearrange("p j n -> p (j n)"), pT[:], [pT])
                    ph.store("pool", st, self.nss[s, l, d].rearrange("(j p) n -> p j n", p=128), st[:])

    def ph_merge(self, l):
        with Phase(self, f"merge{l}") as ph:
            mod = ph.sb("mod", [128, 4, 48, 2], F32)
            ph.load("sp", mod, mod[:], self.MOD)
            wao = ph.sb("wao", [128, 8, 1024], BF16)
            wso = ph.sb("wso", [128, 16, 1024], BF16)
            wo = ph.sb("wo", [128, 8, 1024], BF16)
            self.load_w(ph, wao, lambda o, n: wao[:, :, o:o + n], self.w_attn_o[l], 0, 1024)
            for hf in range(2):
                ph.load("pool", wso, wso[:, 8 * hf:8 * hf + 8, 0:512], self.w_ssd_o[l, 1024 * hf:1024 * (hf + 1), 0:512].rearrange("(c p) n -> p c n", p=128))
                ph.load("pool", wso, wso[:, 8 * hf:8 * hf + 8, 512:1024], self.w_ssd_o[l, 1024 * hf:1024 * (hf + 1), 512:1024].rearrange("(c p) n -> p c n", p=128))
            self.load_w(ph, wo, lambda o, n: wo[:, :, o:o + n], self.w_out[l], 0, 1024)
            at = ph.sb("at", [128, 8, 512], BF16)
            yt = ph.sb("yt", [128, 16, 512], BF16)
            gg = ph.sb("gg", [128, 16, 512], BF16)
            x_ = ph.sb("x", [128, 8, 512], F32)
            mm_ = ph.sb("m", [128, 8, 512], BF16)
            m1 = ph.sb("m1", [128, 512], F32)
            m2 = ph.sb("m2", [128, 512], F32)
            pa = [ph.ps(f"pa{i}", [128, 512]) for i in range(2)]
            pss = [ph.ps(f"pss{i}", [128, 512]) for i in range(2)]
            po = [ph.ps(f"po{i}", [128, 512]) for i in range(2)]
            nb = self.norm_bufs(ph)
            for t in range(NT):
                cond = 0 if t < 8 else 1
                tsl = slice(t * 512, (t + 1) * 512)
                ph.load("sp", at, at[:], self.AT[:, :, tsl].rearrange("c p t -> p c t"))
                ph.load("sp", yt, yt[:, 0:8, :], self.yT[0:8, :, tsl].rearrange("c p t -> p c t"))
                ph.load("sp", yt, yt[:, 8:16, :], self.yT[8:16, :, tsl].rearrange("c p t -> p c t"))
                ph.load("sp", gg, gg[:, 0:8, :], self.GA[0:8, :, tsl].rearrange("c p t -> p c t"))
                ph.load("sp", gg, gg[:, 8:16, :], self.GA[8:16, :, tsl].rearrange("c p t -> p c t"))
                ph.load("sp", x_, x_[:], self.xT[:, :, tsl].rearrange("c p t -> p c t"))
                for j in range(8):
                    pa_, ps_ = pa[j % 2], pss[j % 2]
                    for k in range(8):
                        ph.mm(pa_, pa_[:], wao[:, k, j * 128:(j + 1) * 128], at[:, k, :], k == 0, k == 7, [wao, at])
                    for k in range(16):
                        ph.mm(ps_, ps_[:], wso[:, k, j * 128:(j + 1) * 128], yt[:, k, :], k == 0, k == 15, [wso, yt])
                    ph.tt(m1, m1[:], pa_[:], gg[:, j, :], ALU.mult, [pa_, gg])
                    ph.tt(m2, m2[:], ps_[:], gg[:, 8 + j, :], ALU.mult, [ps_, gg])
                    ph.tt(mm_, mm_[:, j, :], m1[:], m2[:], ALU.add, [m1, m2])
                for j2 in range(8):
                    po_ = po[j2 % 2]
                    for j in range(8):
                        ph.mm(po_, po_[:], wo[:, j, j2 * 128:(j2 + 1) * 128], mm_[:, j, :], j == 0, j == 7, [wo, mm_])
                    ph.stt(x_, x_[:, j2, :], po_[:], mod[:, l, 16 + j2, cond:cond + 1], x_[:, j2, :], ALU.mult, ALU.add, [po_, mod, x_])
                ph.store("pool", x_, self.xT[:, :, tsl].rearrange("c p t -> p c t"), x_[:])
                self.emit_norm(ph, nb, mod, x_, l, 1, t)

    def ph_ffn_up(self, l):
        with Phase(self, f"up{l}") as ph:
            cw = ph.sb("cw", [128, 44, 3], F32)
            ph.load("sp", cw, cw[:], self.ffn_cw[:, l, :, :])
            cb = ph.sb("cb", [128, 44], F32)
            ph.load("sp", cb, cb[:], self.ffn_cb[:, l, :])
            ws = [ph.sb(f"ws{i}", [128, 8, 256], BF16) for i in range(3)]
            hres = [ph.sb(f"hr{i}", [128, 8, 512], BF16) for i in range(NT)]
            for t in range(NT):
                ph.load("sp", hres[t], hres[t][:], self.hT[:, :, t * 512:(t + 1) * 512].rearrange("c p t -> p c t"))
            pc = [ph.ps(f"pc{i}", [128, 512]) for i in range(6)]
            raws = [ph.sb(f"raw{i}", [128, RAWN], F32) for i in range(4)]
            for r_ in raws:
                ph.op("pool", lambda e, r_=r_: e.memset(r_[:], 0.0), [], [r_])
            HN = (RAWN - 2) // 2
            accv = ph.sb("accv", [128, HN], F32)
            accg = ph.sb("accg", [128, HN], F32)
            aos = [ph.sb(f"ao{i}", [128, RAWN], BF16) for i in range(2)]
            n = RAWN - 2
            NS = 22
            state = dict(h=0, p=0)

            def load_slab(i):
                w_ = ws[i % 3]
                ph.load("pool", w_, w_[:, :, 0:128], self.w_up[l][:, 128 * i:128 * (i + 1)].rearrange("(c p) n -> p c n", p=128))
                ph.load("pool", w_, w_[:, :, 128:256], self.w_up[l][:, 2816 + 128 * i:2816 + 128 * (i + 1)].rearrange("(c p) n -> p c n", p=128))

            def mm_slab(si):
                w_ = ws[si % 3]
                for t in range(NT):
                    ht = hres[t]
                    for cj in range(2):
                        p_ = pc[state["p"] % 6]
                        state["p"] += 1
                        for kc in range(8):
                            ph.mm(p_, p_[:], w_[:, kc, cj * 128:(cj + 1) * 128], ht[:, kc, :], kc == 0, kc == 7, [w_, ht])
                        self.evac_raw(ph, raws[2 * (si % 2) + cj], p_, t, eng="act")

            def post_slab(si):
                chv = si
                chg = 22 + si
                ao = aos[si % 2]
                rv, rg = raws[2 * (si % 2)], raws[2 * (si % 2) + 1]
                for a0 in (0, HN):
                    self.conv_chunk(ph, rg, accg, cw, None, [cw[:, chg, i:i + 1] for i in range(3)], a0=a0, n=HN)
                    ph.act(accg, accg[:, 0:HN], accg[:, 0:HN], AF.Silu, [accg, cb], bias=cb[:, chg:chg + 1])
                    self.conv_chunk(ph, rv, accv, cw, None, [cw[:, chv, i:i + 1] for i in range(3)], a0=a0, n=HN)
                    ph.stt(ao, ao[:, a0:a0 + HN], accv[:, 0:HN], cb[:, chv:chv + 1], accg[:, 0:HN], ALU.add, ALU.mult, [accv, cb, accg])
                for (tok0, a0, nn) in SEGS:
                    ph.store("pool", ao, self.actT[chv, :, tok0:tok0 + nn], ao[:, a0:a0 + nn])

            load_slab(0)
            load_slab(1)
            mm_slab(0)
            for si in range(NS):
                if si + 2 < NS:
                    load_slab(si + 2)
                if si + 1 < NS:
                    mm_slab(si + 1)
                post_slab(si)

    def ph_ffn_down(self, l):
        with Phase(self, f"down{l}") as ph:
            mod = ph.sb("mod", [128, 4, 48, 2], F32)
            ph.load("sp", mod, mod[:], self.MOD)
            wd = ph.sb("wd", [128, 22, 1024], BF16)
            for c0 in range(0, 22, 8):
                ncn = min(8, 22 - c0)
                for hf in range(2):
                    ph.load("pool", wd, wd[:, c0:c0 + ncn, 512 * hf:512 * (hf + 1)],
                            self.w_down[l, c0 * 128:(c0 + ncn) * 128, 512 * hf:512 * (hf + 1)].rearrange("(c p) n -> p c n", p=128))
            acts = [ph.sb(f"act{i}", [128, 22, 512], BF16) for i in range(2)]
            xs = [ph.sb(f"x{i}", [128, 8, 512], F32) for i in range(2)]
            po = [ph.ps(f"po{i}", [128, 512]) for i in range(2)]
            nb = self.norm_bufs(ph) if l + 1 < DEPTH else None
            for t in range(NT):
                cond = 0 if t < 8 else 1
                tsl = slice(t * 512, (t + 1) * 512)
                a_, x_ = acts[t % 2], xs[t % 2]
                ph.load("sp", a_, a_[:, 0:11, :], self.actT[0:11, :, tsl].rearrange("c p t -> p c t"))
                ph.load("sp", a_, a_[:, 11:22, :], self.actT[11:22, :, tsl].rearrange("c p t -> p c t"))
                ph.load("sp", x_, x_[:], self.xT[:, :, tsl].rearrange("c p t -> p c t"))
                for j2 in range(8):
                    po_ = po[j2 % 2]
                    for j in range(22):
                        ph.mm(po_, po_[:], wd[:, j, j2 * 128:(j2 + 1) * 128], a_[:, j, :], j == 0, j == 21, [wd, a_])
                    ph.stt(x_, x_[:, j2, :], po_[:], mod[:, l, 40 + j2, cond:cond + 1], x_[:, j2, :], ALU.mult, ALU.add, [po_, mod, x_])
                ph.store("pool", x_, self.xT[:, :, tsl].rearrange("c p t -> p c t"), x_[:])
                if nb is not None:
                    self.emit_norm(ph, nb, mod, x_, l + 1, 0, t)

    def ph_final(self):
        with Phase(self, "final") as ph:
            c = self.consts(ph, ["mask"])
            m = c["mask"]
            ident = m[:, 4, :]
            xs = [ph.sb(f"x{i}", [128, 8, 512], F32) for i in range(2)]
            ot = [ph.sb(f"ot{i}", [128, 1024], F32) for i in range(2)]
            pt = [ph.ps(f"pt{i}", [128, 512]) for i in range(4)]
            k = 0
            for t in range(NT):
                x_ = xs[t % 2]
                ph.load("sp", x_, x_[:], self.xT[:, :, t * 512:(t + 1) * 512].rearrange("c p t -> p c t"))
                for j in range(4):
                    o_ = ot[k % 2]
                    k += 1
                    for hf in range(2):
                        p_ = pt[(2 * k + hf) % 4]
                        for cc in range(4):
                            cch = 4 * hf + cc
                            ph.tr(p_, p_[:, cc * 128:(cc + 1) * 128], x_[:, cch, j * 128:(j + 1) * 128], ident, [x_, m])
                        ph.cp("act" if hf else "dve", o_, o_[:, hf * 512:(hf + 1) * 512], p_[:], [p_])
                    r0 = (4 * t + j) * 128
                    ph.store("pool", o_, self.y_tok[r0:r0 + 128, :], o_[:])


_CACHE = {}


def _consts():
    half = 64
    inv_freq = (10000.0 ** (-np.arange(0, half, 2, dtype=np.float32) / half)).astype(np.float32)
    pos = np.arange(4096)
    row = (pos // 64).astype(np.float32)
    col = (pos % 64).astype(np.float32)
    ang_r = row[:, None] * inv_freq
    ang_c = col[:, None] * inv_freq
    ang = np.concatenate([ang_r, ang_r, ang_c, ang_c], axis=-1).astype(np.float32)
    rope = np.stack([np.cos(ang).T, np.sin(ang).T]).astype(np.float32)
    k = np.arange(128)[:, None]
    j = np.arange(128)[None, :]
    cm = np.stack([(k <= j), (k >= j), (k > j), (k < j), (k == j), np.ones((128, 128), bool)], axis=1).astype(np.float32)
    prot = np.zeros((128, 128), np.float32)
    for d in range(128):
        q = d % 64
        base = d - q
        if q < 32:
            prot[base + q + 32, d] = -1.0
        else:
            prot[base + q - 32, d] = 1.0
    return rope, np.ascontiguousarray(cm), prot


def kernel(x_prompt, x_sample, c, cache_k, cache_v, state_ssd, c_ctx, w_mod, b_mod, w_in, q_norm, k_norm,
           conv_w, conv_b, dt_bias, a_log, d_skip, ssd_norm, w_attn_o, w_ssd_o, w_out, w_up,
           ffn_conv_w, ffn_conv_b, w_down):
    f = lambda a: np.ascontiguousarray(np.asarray(a, dtype=np.float32))
    if "nc" not in _CACHE:
        _CACHE["nc"] = Kernel().nc
    nc = _CACHE["nc"]
    rope, cm, prot = _consts()
    x_prompt, x_sample = f(x_prompt), f(x_sample)
    shared = {
        "w_mod": f(w_mod), "w_in": f(w_in), "w_attn_o": f(w_attn_o), "w_ssd_o": f(w_ssd_o), "w_out": f(w_out),
        "w_up": f(w_up), "w_down": f(w_down),
        "b_mod": f(np.asarray(b_mod).reshape(4, 48, 128).transpose(2, 0, 1)),
        "qkn": f(np.stack([np.asarray(q_norm), np.asarray(k_norm)], axis=-1).transpose(1, 0, 2)),
        "conv_w": f(np.asarray(conv_w).reshape(4, 3, 24, 128).transpose(3, 0, 2, 1)),
        "conv_b": f(np.asarray(conv_b).reshape(4, 24, 128).transpose(2, 0, 1)),
        "dt_bias": f(np.asarray(dt_bias).reshape(4, 64)), "a_log": f(np.asarray(a_log).reshape(4, 64)),
        "d_skip": f(d_skip), "ssd_norm": f(ssd_norm),
        "ffn_cw": f(np.asarray(ffn_conv_w).reshape(4, 3, 44, 128).transpose(3, 0, 2, 1)),
        "ffn_cb": f(np.asarray(ffn_conv_b).reshape(4, 44, 128).transpose(2, 0, 1)),
        "rope": rope, "cmask": cm, "prot": prot,
    }
    in_maps = []
    for core in range(8):
        b = core // 4
        d = dict(shared)
        d["x_tok"] = np.ascontiguousarray(np.concatenate([x_sample[b], x_prompt[2 * core], x_prompt[2 * core + 1]], axis=0))
        cv = np.stack([np.asarray(c)[b], np.asarray(c_ctx)], axis=-1)
        d["cvec"] = f(cv.reshape(8, 128, 2).transpose(1, 0, 2))
        d["cache_k"] = f(np.asarray(cache_k)[b].reshape(4, 256, 256))
        d["cache_v"] = f(np.asarray(cache_v)[b].reshape(4, 256, 256))
        d["state"] = f(np.asarray(state_ssd)[b].reshape(4, 2, 2048, 128))
        in_maps.append(d)
    res = run_bass_kernel_spmd(nc, in_maps, core_ids=list(range(8)))
    R = res.results
    y_prompt = np.empty((16, 256, 1024), np.float32)
    y_sample = np.empty((2, 4096, 1024), np.float32)
    nk = np.empty((16, 4, 256, 2, 128), np.float32)
    nv = np.empty((16, 4, 256, 2, 128), np.float32)
    ns = np.empty((16, 4, 2, 32, 64, 128), np.float32)
    for core in range(8):
        r = R[core]
        yt = np.asarray(r["y_tok"])
        if core % 4 == 0:
            y_sample[core // 4] = yt[0:4096]
        for s in range(2):
            y_prompt[2 * core + s] = yt[4096 + 256 * s:4096 + 256 * (s + 1)]
            nk[2 * core + s] = np.asarray(r["nck"])[s].reshape(4, 256, 2, 128)
            nv[2 * core + s] = np.asarray(r["ncv"])[s].reshape(4, 256, 2, 128)
            ns[2 * core + s] = np.asarray(r["nss"])[s].reshape(4, 2, 32, 64, 128)
    return (y_prompt, y_sample, nk, nv, ns)
```

```python
import numpy as np
import ml_dtypes
from contextlib import ExitStack
import concourse.bass as bass
import concourse.mybir as mybir
from concourse.bass_utils import run_bass_kernel_spmd

F32 = mybir.dt.float32
BF16 = mybir.dt.bfloat16
AF = mybir.ActivationFunctionType
ALU = mybir.AluOpType
SEM_LIMIT = 30000

DEPTH = 4
T = 4608
NT = 9
NB = 36
EPS = 1e-6
SEQS = [(0, 4096, 0), (4096, 256, 1), (4352, 256, 2)]
RAWN = 4612
SEGS = [(0, 0, 4096), (4096, 4097, 256), (4352, 4354, 256)]


class Res:
    __slots__ = ("name", "ws", "rs")

    def __init__(self, name):
        self.name = name
        self.ws = {}
        self.rs = {}


class Slot:
    __slots__ = ("name", "last", "n", "pool", "base", "kind")

    def __init__(self, name, kind):
        self.name = name
        self.kind = kind
        self.last = None
        self.n = 0
        self.pool = None
        self.base = 0


class Op:
    __slots__ = ("eng", "fn", "deps", "signal", "cnt", "slot", "stream", "sz")


class GlobalSems:
    def __init__(self, nc, es, n_eng=4, n_dma=72):
        self.engs = ("pe", "act", "dve", "pool", "sp")
        self.esem = {e: [es.enter_context(nc.semaphore(f"s_{e}{i}")) for i in range(n_eng)] for e in self.engs}
        self.ecnt = {e: 0 for e in self.engs}
        self.dsem = [es.enter_context(nc.semaphore(f"d{i}")) for i in range(n_dma)]
        self.dval = [0] * n_dma
        self.dkind = ["hw" if i < n_dma // 2 else "sw" for i in range(n_dma)]


class Sched:
    ENGS = ("pe", "act", "dve", "pool", "sp")

    def __init__(self, nc, gs):
        self.nc = nc
        self.gs = gs
        self.ops = {e: [] for e in self.ENGS}
        self.slots = []

    def slot(self, name, kind):
        s = Slot(name, kind)
        self.slots.append(s)
        return s

    def add(self, eng, fn, reads=(), writes=(), slot=None, sz=0):
        op = Op()
        op.sz = sz
        op.eng = eng
        op.fn = fn
        op.signal = False
        op.cnt = None
        op.slot = slot
        op.stream = slot if slot is not None else eng
        deps = {}

        def dep(d):
            if d is None or d is op:
                return
            if d.slot is None and slot is None and d.eng == eng:
                if eng == "pe":
                    return
                if eng in ("act", "dve") and d.sz >= 256:
                    return
            deps[id(d)] = d

        for r in reads:
            for d in r.ws.values():
                dep(d)
        for w in writes:
            for d in w.ws.values():
                dep(d)
            for d in w.rs.values():
                dep(d)
        if slot is not None:
            dep(slot.last)
            slot.last = op
            slot.n += 1
            op.cnt = slot.n
        for d in deps.values():
            d.signal = True
        op.deps = list(deps.values())
        for w in writes:
            w.ws = {op.stream: op}
            w.rs = {}
        for r in reads:
            r.rs[op.stream] = op
        self.ops[eng].append(op)
        return op

    def emit(self):
        nc, gs = self.nc, self.gs
        for e in self.ENGS:
            cops = [o for o in self.ops[e] if o.slot is None]
            if cops:
                cops[-1].signal = True
        final = {}
        for e in self.ENGS:
            c = gs.ecnt[e]
            for op in self.ops[e]:
                if op.slot is None and op.signal:
                    c += 1
                    op.cnt = c
            final[e] = c if c > gs.ecnt[e] else None
            gs.ecnt[e] = c
        used = [s for s in self.slots if s.n > 0]
        for kind in ("hw", "sw"):
            uk = [s for s in used if s.kind == kind]
            order = sorted([i for i in range(len(gs.dsem)) if gs.dkind[i] == kind], key=lambda i: gs.dval[i])
            assert len(uk) <= len(order), f"too many dma slots {len(uk)}"
            for s, i in zip(sorted(uk, key=lambda s: -s.n), order):
                s.pool = i
                s.base = gs.dval[i]
                gs.dval[i] += 16 * s.n
                assert gs.dval[i] < 60000, "dma semaphore overflow"

        nE = len(gs.esem["pe"])

        def target(d):
            if d.slot is not None:
                return gs.dsem[d.slot.pool], d.slot.base + 16 * d.cnt
            i = d.cnt - 1
            ep = i // SEM_LIMIT
            assert ep < nE, "engine semaphore overflow"
            return gs.esem[d.eng][ep], (i % SEM_LIMIT) + 1

        with nc.Block() as block:
            def run(e, eng):
                known = {}

                def wait(sem, val):
                    k = id(sem)
                    if known.get(k, 0) >= val:
                        return
                    known[k] = val
                    eng.wait_ge(sem, val)

                for op in self.ops[e]:
                    for d in op.deps:
                        wait(*target(d))
                    ins = op.fn(eng)
                    if op.slot is not None:
                        sem, val = target(op)
                        ins.then_inc(sem, 16)
                    elif op.signal:
                        sem, val = target(op)
                        ins.then_inc(sem, 1)
                for e2 in self.ENGS:
                    if final[e2] is not None:
                        i = final[e2] - 1
                        wait(gs.esem[e2][i // SEM_LIMIT], (i % SEM_LIMIT) + 1)
                for s in used:
                    wait(gs.dsem[s.pool], s.base + 16 * s.n)

            @block.tensor
            def _(eng):
                run("pe", eng)

            @block.scalar
            def _(eng):
                run("act", eng)

            @block.vector
            def _(eng):
                run("dve", eng)

            @block.gpsimd
            def _(eng):
                run("pool", eng)

            @block.sync
            def _(eng):
                run("sp", eng)


class Tl:
    def __init__(self, ph, t, name):
        self.t = t
        self.r = Res(name)
        self.ph = ph
        self.name = name
        self._slot = {}

    def __getitem__(self, k):
        return self.t[k]

    def slot(self, q):
        kind = "hw" if q == "sp" else "sw"
        if kind not in self._slot:
            self._slot[kind] = self.ph.S.slot(self.name + kind, kind)
        return self._slot[kind]


class Phase:
    def __init__(self, K, name):
        self.K = K
        self.nc = K.nc
        self.name = name
        self.es = ExitStack()
        self.S = Sched(K.nc, K.gs)
        self.dr = {}
        self.n = 0

    def __enter__(self):
        self.es.__enter__()
        return self

    def __exit__(self, *a):
        if a[0] is None:
            self.S.emit()
        return self.es.__exit__(*a)

    def sb(self, name, shape, dt):
        self.n += 1
        t = self.es.enter_context(self.nc.sbuf_tensor(f"{self.name}_{name}_{self.n}", list(shape), dt))
        return Tl(self, t, name)

    def ps(self, name, shape, dt=F32):
        self.n += 1
        t = self.es.enter_context(self.nc.psum_tensor(f"{self.name}_{name}_{self.n}", list(shape), dt))
        return Tl(self, t, name)

    def dres(self, key):
        r = self.dr.get(key)
        if r is None:
            r = Res(str(key))
            self.dr[key] = r
        return r

    def op(self, eng, fn, rd=(), wr=(), sz=0):
        rd = [x.r if isinstance(x, Tl) else x for x in rd]
        wr = [x.r if isinstance(x, Tl) else x for x in wr]
        return self.S.add(eng, fn, rd, wr, sz=sz)

    def load(self, q, tile, out_ap, in_ap, rd=(), extra_wr=()):
        rd = [x.r if isinstance(x, Tl) else x for x in rd]
        wr = [tile.r] + [x.r if isinstance(x, Tl) else x for x in extra_wr]
        return self.S.add(q, lambda e: e.dma_start(out=out_ap, in_=in_ap), rd, wr, slot=tile.slot(q))

    def store(self, q, tile, out_ap, in_ap, wr=()):
        wr = [x.r if isinstance(x, Tl) else x for x in wr]
        return self.S.add(q, lambda e: e.dma_start(out=out_ap, in_=in_ap), [tile.r], wr, slot=tile.slot(q))

    def mm(self, out_t, out_ap, lhsT_ap, rhs_ap, start, stop, rd):
        return self.op("pe", lambda e: e.matmul(out_ap, lhsT=lhsT_ap, rhs=rhs_ap, start=start, stop=stop), rd, [out_t])

    def tr(self, out_t, out_ap, in_ap, ident_ap, rd):
        return self.op("pe", lambda e: e.transpose(out_ap, in_ap, ident_ap), rd, [out_t])

    def act(self, out_t, out_ap, in_ap, func, rd, bias=None, scale=None, accum=None, extra_wr=()):
        kw = {}
        if bias is not None:
            kw["bias"] = bias
        if scale is not None:
            kw["scale"] = scale
        if accum is not None:
            kw["accum_out"] = accum
        return self.op("act", lambda e: e.activation(out=out_ap, in_=in_ap, func=func, **kw), rd, [out_t] + list(extra_wr),
                       sz=0 if accum is not None else out_ap.free_size())

    def tt(self, out_t, out_ap, in0, in1, op, rd, eng="dve"):
        return self.op(eng, lambda e: e.tensor_tensor(out=out_ap, in0=in0, in1=in1, op=op), rd, [out_t], sz=out_ap.free_size())

    def stt(self, out_t, out_ap, in0, scalar, in1, op0, op1, rd):
        return self.op("dve", lambda e: e.scalar_tensor_tensor(out=out_ap, in0=in0, scalar=scalar, in1=in1, op0=op0, op1=op1), rd, [out_t], sz=out_ap.free_size())

    def tsm(self, out_t, out_ap, in0, scalar, rd, eng="dve"):
        return self.op(eng, lambda e: e.tensor_scalar_mul(out=out_ap, in0=in0, scalar1=scalar), rd, [out_t], sz=out_ap.free_size())

    def cp(self, eng, out_t, out_ap, in_ap, rd):
        if eng == "act":
            return self.op("act", lambda e: e.copy(out=out_ap, in_=in_ap), rd, [out_t], sz=out_ap.free_size())
        return self.op(eng, lambda e: e.tensor_copy(out=out_ap, in_=in_ap), rd, [out_t], sz=out_ap.free_size())

    def rsqrt(self, out_t, out_ap, in_ap, rd, scale=None, tmp_t=None, tmp_ap=None, power=-0.5):
        tt_, ta = (tmp_t, tmp_ap) if tmp_t is not None else (out_t, out_ap)
        self.act(tt_, ta, in_ap, AF.Ln, rd, bias=EPS, scale=scale)
        return self.act(out_t, out_ap, ta, AF.Exp, [tt_], scale=power)


def bc(ap, shape):
    return ap.to_broadcast(list(shape))


class Kernel:
    def __init__(self):
        nc = bass.Bass("TRN2", target_bir_lowering=False)
        self.nc = nc
        D = nc.dram_tensor

        def inp(name, shape, dt=F32):
            return D(name, list(shape), dt, kind="ExternalInput").ap()

        def outp(name, shape):
            return D(name, list(shape), F32, kind="ExternalOutput").ap()

        def scr(name, shape, dt):
            return D(name, list(shape), dt).ap()

        self.x_tok = inp("x_tok", [T, 1024])
        self.cvec = inp("cvec", [128, 8, 2])
        self.cache_k = inp("cache_k", [4, 256, 256])
        self.cache_v = inp("cache_v", [4, 256, 256])
        self.state = inp("state", [4, 2, 2048, 128])
        self.w_mod = inp("w_mod", [4, 1024, 6144])
        self.b_mod = inp("b_mod", [128, 4, 48])
        self.w_in = inp("w_in", [4, 1024, 8768])
        self.qkn = inp("qkn", [128, 4, 2])
        self.conv_w = inp("conv_w", [128, 4, 24, 3])
        self.conv_b = inp("conv_b", [128, 4, 24])
        self.dt_bias = inp("dt_bias", [4, 64])
        self.a_log = inp("a_log", [4, 64])
        self.d_skip = inp("d_skip", [4, 32])
        self.ssd_norm = inp("ssd_norm", [4, 2048])
        self.w_attn_o = inp("w_attn_o", [4, 1024, 1024])
        self.w_ssd_o = inp("w_ssd_o", [4, 2048, 1024])
        self.w_out = inp("w_out", [4, 1024, 1024])
        self.w_up = inp("w_up", [4, 1024, 5632])
        self.ffn_cw = inp("ffn_cw", [128, 4, 44, 3])
        self.ffn_cb = inp("ffn_cb", [128, 4, 44])
        self.w_down = inp("w_down", [4, 2816, 1024])
        self.rope = inp("rope", [2, 128, 4096])
        self.cmask = inp("cmask", [128, 6, 128])
        self.prot = inp("prot", [128, 128])

        self.y_tok = outp("y_tok", [T, 1024])
        self.nck = outp("nck", [2, 4, 256, 256])
        self.ncv = outp("ncv", [2, 4, 256, 256])
        self.nss = outp("nss", [2, 4, 2, 2048, 128])

        self.xT = scr("xT", [8, 128, T], F32)
        self.hT = scr("hT", [8, 128, T], BF16)
        self.MOD = scr("MOD", [128, 4, 48, 2], F32)
        self.qT = scr("qT", [8, 128, T], BF16)
        self.kT = scr("kT", [2, 128, T], BF16)
        self.Vt = scr("Vt", [T, 256], BF16)
        self.Zs = scr("Zs", [T, 2048], BF16)
        self.DT = scr("DT", [T, 64], F32)
        self.XS = scr("XS", [T, 2048], BF16)
        self.Bt = scr("Bt", [T, 512], BF16)
        self.BT = scr("BT", [4, 128, T], BF16)
        self.CT = scr("CT", [4, 128, T], BF16)
        self.GA = scr("GA", [16, 128, T], BF16)
        self.AT = scr("AT", [8, 128, T], BF16)
        self.Y1 = scr("Y1", [T, 2048], F32)
        self.yT = scr("yT", [16, 128, T], BF16)
        self.actT = scr("actT", [22, 128, T], BF16)

        with ExitStack() as es:
            self.gs = GlobalSems(nc, es)
            self.build()

    def build(self):
        self.ph_init()
        self.ph_norm(0, 0)
        for l in range(DEPTH):
            self.ph_inproj_tok(l)
            self.ph_inproj_feat(l, 0)
            self.ph_inproj_feat(l, 1)
            self.ph_attn(l)
            self.ph_ssd(l, 0)
            self.ph_ssd(l, 1)
            self.ph_merge(l)
            self.ph_ffn_up(l)
            self.ph_ffn_down(l)
        self.ph_final()

    def consts(self, ph, want):
        c = {}
        if "mask" in want:
            m = ph.sb("cmask", [128, 6, 128], F32)
            ph.load("sp", m, m[:], self.cmask)
            c["mask"] = m
        if "maskb" in want:
            mb = ph.sb("cmaskb", [128, 6, 128], BF16)
            ph.load("pool", mb, mb[:], self.cmask)
            c["maskb"] = mb
        return c

    def ph_init(self):
        with Phase(self, "init") as ph:
            c = self.consts(ph, ["mask"])
            m = c["mask"]
            ident = m[:, 4, :]
            xt = [ph.sb(f"xt{i}", [128, 4, 1024], F32) for i in range(2)]
            st = [ph.sb(f"st{i}", [128, 8, 512], F32) for i in range(2)]
            pt = [ph.ps(f"pt{i}", [128, 512]) for i in range(4)]
            xv = self.x_tok.rearrange("(t j p) f -> t p j f", j=4, p=128)
            for t in range(NT):
                x_ = xt[t % 2]
                s_ = st[t % 2]
                ph.load("sp", x_, x_[:], xv[t])
                for cch in range(8):
                    p_ = pt[cch % 4]
                    for j in range(4):
                        ph.tr(p_, p_[:, j * 128:(j + 1) * 128], x_[:, j, cch * 128:(cch + 1) * 128], ident, [x_, m])
                    ph.cp("act" if cch % 2 else "dve", s_, s_[:, cch, :], p_[:], [p_])
                ph.store("pool", s_, self.xT[:, :, t * 512:(t + 1) * 512].rearrange("c p t -> p c t"), s_[:])
            cv = ph.sb("cv", [128, 8, 2], F32)
            ph.load("sp", cv, cv[:], self.cvec)
            ph.act(cv, cv[:], cv[:], AF.Silu, [cv])
            bm = ph.sb("bm", [128, 4, 48], F32)
            ph.load("sp", bm, bm[:], self.b_mod)
            modt = ph.sb("modt", [128, 4, 48, 2], F32)
            wm = [ph.sb(f"wm{i}", [128, 8, 1024], F32) for i in range(2)]
            pm = ph.ps("pm", [128, 96])
            k = 0
            for l in range(DEPTH):
                for s in range(6):
                    w_ = wm[k % 2]
                    k += 1
                    ph.load("sp", w_, w_[:], self.w_mod[l, :, s * 1024:(s + 1) * 1024].rearrange("(c p) n -> p c n", p=128))
                    for j in range(8):
                        col = (s * 8 + j) * 2
                        for kc in range(8):
                            ph.mm(pm, pm[:, col:col + 2], w_[:, kc, j * 128:(j + 1) * 128], cv[:, kc, :], kc == 0, kc == 7, [w_, cv])
                ph.tt(modt, modt[:, l, :, :], pm[:].rearrange("p (c r) -> p c r", r=2),
                      bc(bm[:, l, :].unsqueeze(2), [128, 48, 2]), ALU.add, [pm, bm])
                for c0 in (8, 32):
                    ph.op("dve", lambda e, l=l, c0=c0: e.tensor_scalar_add(out=modt[:, l, c0:c0 + 8, :], in0=modt[:, l, c0:c0 + 8, :], scalar1=1.0), [modt], [modt])
            ph.store("pool", modt, self.MOD, modt[:])

    def ph_norm(self, l, k):
        with Phase(self, f"norm{l}_{k}") as ph:
            mod = ph.sb("mod", [128, 4, 48, 2], F32)
            ph.load("sp", mod, mod[:], self.MOD)
            ones = ph.sb("ones", [128, 128], BF16)
            ph.op("dve", lambda e: e.memset(ones[:], 1.0 / 1024.0), [], [ones])
            xs = [ph.sb(f"x{i}", [128, 8, 512], F32) for i in range(2)]
            sq = ph.sb("sq", [128, 8, 512], BF16)
            hs = [ph.sb(f"h{i}", [128, 8, 512], BF16) for i in range(2)]
            r1 = ph.sb("r1", [128, 512], F32)
            rstd = ph.sb("rstd", [128, 512], F32)
            pms = [ph.ps(f"pms{i}", [128, 512]) for i in range(2)]
            for t in range(NT):
                cond = 0 if t < 8 else 1
                x_, h_, pm = xs[t % 2], hs[t % 2], pms[t % 2]
                ph.load("sp", x_, x_[:], self.xT[:, :, t * 512:(t + 1) * 512].rearrange("c p t -> p c t"))
                ph.act(sq, sq[:], x_[:], AF.Square, [x_])
                for cch in range(8):
                    ph.mm(pm, pm[:], ones[:], sq[:, cch, :], cch == 0, cch == 7, [ones, sq])
                ph.rsqrt(rstd, rstd[:], pm[:], [pm], tmp_t=r1, tmp_ap=r1[:])
                ph.tt(x_, x_[:], x_[:], bc(rstd[:].unsqueeze(1), [128, 8, 512]), ALU.mult, [x_, rstd])
                for cch in range(8):
                    ph.act(h_, h_[:, cch, :], x_[:, cch, :], AF.Identity, [x_, mod],
                           scale=mod[:, l, 24 * k + 8 + cch, cond:cond + 1], bias=mod[:, l, 24 * k + cch, cond:cond + 1])
                ph.store("pool", h_, self.hT[:, :, t * 512:(t + 1) * 512].rearrange("c p t -> p c t"), h_[:])

    def norm_bufs(self, ph):
        ones = ph.sb("nones", [128, 128], BF16)
        ph.op("dve", lambda e: e.memset(ones[:], 1.0 / 1024.0), [], [ones])
        return dict(ones=ones, sq=ph.sb("nsq", [128, 8, 512], BF16), tmp=ph.sb("ntmp", [128, 8, 512], F32),
                    hs=[ph.sb(f"nh{i}", [128, 8, 512], BF16) for i in range(2)], r1=ph.sb("nr1", [128, 512], F32),
                    rstd=ph.sb("nrstd", [128, 512], F32), pm=ph.ps("npm", [128, 512]))

    def emit_norm(self, ph, nb, mod, x_, l, k, t):
        cond = 0 if t < 8 else 1
        sq, tmp, h_, pm, rstd = nb["sq"], nb["tmp"], nb["hs"][t % 2], nb["pm"], nb["rstd"]
        ph.act(sq, sq[:], x_[:], AF.Square, [x_])
        for cch in range(8):
            ph.mm(pm, pm[:], nb["ones"][:], sq[:, cch, :], cch == 0, cch == 7, [nb["ones"], sq])
        ph.rsqrt(rstd, rstd[:], pm[:], [pm], tmp_t=nb["r1"], tmp_ap=nb["r1"][:])
        ph.tt(tmp, tmp[:], x_[:], bc(rstd[:].unsqueeze(1), [128, 8, 512]), ALU.mult, [x_, rstd])
        for cch in range(8):
            ph.act(h_, h_[:, cch, :], tmp[:, cch, :], AF.Identity, [tmp, mod],
                   scale=mod[:, l, 24 * k + 8 + cch, cond:cond + 1], bias=mod[:, l, 24 * k + cch, cond:cond + 1])
        ph.store("pool", h_, self.hT[:, :, t * 512:(t + 1) * 512].rearrange("c p t -> p c t"), h_[:])

    def load_w(self, ph, tile, dst_ap_fn, src, c0, ncols, piece=512):
        o = 0
        while o < ncols:
            n = min(piece, ncols - o)
            ph.load("pool", tile, dst_ap_fn(o, n), src[:, c0 + o:c0 + o + n].rearrange("(c p) n -> p c n", p=128))
            o += n

    def ph_inproj_tok(self, l):
        with Phase(self, f"ipt{l}") as ph:
            wt = ph.sb("wt", [128, 8, 2368], BF16)
            self.load_w(ph, wt, lambda o, n: wt[:, :, o:o + n], self.w_in[l], 1280, 2304)
            self.load_w(ph, wt, lambda o, n: wt[:, :, 2304 + o:2304 + o + n], self.w_in[l], 6656, 64)
            dtb = ph.sb("dtb", [128, 64], F32)
            ph.load("sp", dtb, dtb[:], self.dt_bias[l:l + 1, :].partition_broadcast(128))
            hts = [ph.sb(f"ht{i}", [128, 8, 512], BF16) for i in range(2)]
            pz = [ph.ps(f"pz{i}", [128, 512]) for i in range(4)]
            pv = ph.ps("pv", [128, 256])
            pd = ph.ps("pd", [128, 64])
            vb = [ph.sb(f"vb{i}", [128, 256], BF16) for i in range(2)]
            v32 = [ph.sb(f"v32{i}", [128, 256], F32) for i in range(2)]
            zs = [ph.sb(f"zs{i}", [128, 2048], BF16) for i in range(2)]
            d1 = [ph.sb(f"d1{i}", [128, 64], F32) for i in range(2)]
            for t in range(NT):
                ht = hts[t % 2]
                ph.load("sp", ht, ht[:], self.hT[:, :, t * 512:(t + 1) * 512].rearrange("c p t -> p c t"))
                for j in range(4):
                    blk = 4 * t + j
                    r0 = blk * 128
                    i2 = blk % 2
                    hsl = lambda kc: ht[:, kc, j * 128:(j + 1) * 128]
                    for kc in range(8):
                        ph.mm(pv, pv[:], hsl(kc), wt[:, kc, 0:256], kc == 0, kc == 7, [ht, wt])
                    for g in range(4):
                        for kc in range(8):
                            ph.mm(pz[g], pz[g][:], hsl(kc), wt[:, kc, 256 + g * 512:256 + (g + 1) * 512], kc == 0, kc == 7, [ht, wt])
                    for kc in range(8):
                        ph.mm(pd, pd[:], hsl(kc), wt[:, kc, 2304:2368], kc == 0, kc == 7, [ht, wt])
                    ph.cp("act", vb[i2], vb[i2][:], pv[:], [pv])
                    ph.store("pool", vb[i2], self.Vt[r0:r0 + 128, :], vb[i2][:])
                    if blk >= 32:
                        s = (blk - 32) // 2
                        tr0 = ((blk - 32) % 2) * 128
                        ph.cp("dve", v32[i2], v32[i2][:], pv[:], [pv])
                        ph.store("pool", v32[i2], self.ncv[s, l, tr0:tr0 + 128, :], v32[i2][:])
                    for g in range(4):
                        ph.act(zs[i2], zs[i2][:, g * 512:(g + 1) * 512], pz[g][:], AF.Silu, [pz[g]])
                    ph.store("pool", zs[i2], self.Zs[r0:r0 + 128, :], zs[i2][:])
                    ph.tt(d1[i2], d1[i2][:], pd[:], dtb[:], ALU.add, [pd, dtb])
                    ph.act(d1[i2], d1[i2][:], d1[i2][:], AF.Exp, [d1[i2]])
                    ph.act(d1[i2], d1[i2][:], d1[i2][:], AF.Ln, [d1[i2]], bias=1.0)
                    ph.store("pool", d1[i2], self.DT[r0:r0 + 128, :], d1[i2][:])

    def conv_chunk(self, ph, raw, acc, cw, cb_ap, w_aps, a0=0, n=RAWN - 2):
        ph.tsm(acc, acc[:, 0:n], raw[:, a0:a0 + n], w_aps[0], [raw, cw])
        ph.stt(acc, acc[:, 0:n], raw[:, a0 + 1:a0 + n + 1], w_aps[1], acc[:, 0:n], ALU.mult, ALU.add, [raw, cw, acc])
        ph.stt(acc, acc[:, 0:n], raw[:, a0 + 2:a0 + n + 2], w_aps[2], acc[:, 0:n], ALU.mult, ALU.add, [raw, cw, acc])

    def evac_raw(self, ph, raw, p_, t, eng="act"):
        if t < 8:
            ph.cp(eng, raw, raw[:, 1 + 512 * t:1 + 512 * (t + 1)], p_[:], [p_])
        else:
            ph.cp(eng, raw, raw[:, 4098:4354], p_[:, 0:256], [p_])
            ph.cp(eng, raw, raw[:, 4355:4611], p_[:, 256:512], [p_])

    def ph_inproj_feat(self, l, part):
        with Phase(self, f"ipf{l}_{part}") as ph:
            c = self.consts(ph, ["mask", "maskb"])
            m, mb = c["mask"], c["maskb"]
            identf = m[:, 4, :]
            identb = mb[:, 4, :]
            ones128 = ph.sb("ones128", [128, 128], BF16)
            ph.op("dve", lambda e: e.memset(ones128[:], 1.0 / 128.0), [], [ones128])
            prot = ph.sb("prot", [128, 128], BF16)
            ph.load("pool", prot, prot[:], self.prot)
            qkn = ph.sb("qkn", [128, 4, 2], F32)
            ph.load("sp", qkn, qkn[:], self.qkn)
            cw = ph.sb("cw", [128, 24, 3], F32)
            ph.load("sp", cw, cw[:], self.conv_w[:, l, :, :])
            cb = ph.sb("cb", [128, 24], F32)
            ph.load("sp", cb, cb[:], self.conv_b[:, l, :])
            ws = [ph.sb(f"ws{i}", [128, 8, 512], BF16) for i in range(2)]
            hres = [ph.sb(f"hr{i}", [128, 8, 512], BF16) for i in range(NT)]
            for t in range(NT):
                ph.load("sp", hres[t], hres[t][:], self.hT[:, :, t * 512:(t + 1) * 512].rearrange("c p t -> p c t"))
            pc = [ph.ps(f"pc{i}", [128, 512]) for i in range(4)]
            if part == 0:
                pmn = ph.ps("pmn", [128, 512])
                prr = [ph.ps(f"prr{i}", [128, 512]) for i in range(2)]
            else:
                ptbs = [ph.ps(f"ptb{i}", [128, 1024], BF16) for i in range(2)]
            HN0 = 2304
            if part == 1:
                raws = [ph.sb(f"raw{i}", [128, RAWN], F32) for i in range(4)]
                for r_ in raws:
                    ph.op("pool", lambda e, r_=r_: e.memset(r_[:], 0.0), [], [r_])
                acc = ph.sb("acc", [128, RAWN - 2 - HN0], F32)
                cvos = [ph.sb(f"cvo{i}", [128, RAWN], BF16) for i in range(2)]
                stgs = [ph.sb(f"stg{i}", [128, 12, 128], BF16) for i in range(2)]
            else:
                cs = [ph.sb(f"cos{i}", [128, 2, 512], F32) for i in range(2)]
                sqs = [ph.sb(f"sq{i}", [128, 512], BF16) for i in range(2)]
                rstds = [ph.sb(f"rstd{i}", [128, 512], F32) for i in range(2)]
                qns = [ph.sb(f"qn{i}", [128, 512], F32) for i in range(2)]
                qss = [ph.sb(f"qs{i}", [128, 512], BF16) for i in range(2)]
                t1s = [ph.sb(f"t1{i}", [128, 512], F32) for i in range(2)]
                t2s = [ph.sb(f"t2{i}", [128, 512], F32) for i in range(2)]
                ostg = [ph.sb(f"ostg{i}", [128, 4, 512], BF16) for i in range(2)]
                kf = ph.sb("kf", [128, 512], F32)
                kout = ph.sb("kout", [128, 4, 128], F32)

            slabs = []
            if part == 0:
                slabs += [("q", 0, 4, 0), ("q", 512, 4, 4), ("k", 1024, 2, 0)]
                slabs += [("g", 6720 + 512 * i, 4, 4 * i) for i in range(4)]
            else:
                slabs += [("xbc", 3584 + 256 * i, 2, 2 * i) for i in range(12)]
            state = dict(h=0, k=0, x=0, st=0, tb=0)
            pend = []
            xpar = {}

            def load_slab(i):
                kind, c0, nch, cb0 = slabs[i]
                w_ = ws[i % 2]
                self.load_w(ph, w_, lambda o, n: w_[:, :, o:o + n], self.w_in[l], c0, nch * 128)

            def qk_slab(si):
                kind, c0, nch, cb0 = slabs[si]
                w_ = ws[si % 2]
                nidx = 0 if kind == "q" else 1
                nsc = qkn[:, l, nidx:nidx + 1]
                units = [(t, cj) for t in range(NT) for cj in range(nch)]
                n = len(units)
                hcur = {}

                def A(u):
                    t, cj = units[u]
                    if cj == 0:
                        hcur[t] = hres[t]
                        if t < 8:
                            cs_ = cs[t % 2]
                            ph.load("sp", cs_, cs_[:], self.rope[:, :, t * 512:(t + 1) * 512].rearrange("r p t -> p r t"))
                    ht = hcur[t]
                    p_ = pc[u % 4]
                    for kc in range(8):
                        ph.mm(p_, p_[:], w_[:, kc, cj * 128:(cj + 1) * 128], ht[:, kc, :], kc == 0, kc == 7, [w_, ht])
                    sq = sqs[u % 2]
                    ph.act(sq, sq[:], p_[:], AF.Square, [p_])

                def B(u):
                    t, cj = units[u]
                    p_ = pc[u % 4]
                    kk = u % 2
                    sq, rstd, qn, qs = sqs[kk], rstds[kk], qns[kk], qss[kk]
                    os_ = ostg[t % 2]
                    ph.mm(pmn, pmn[:], ones128[:], sq[:], True, True, [ones128, sq])
                    ph.rsqrt(rstd, rstd[:], pmn[:], [pmn])
                    ph.tt(qn, qn[:], p_[:], rstd[:], ALU.mult, [p_, rstd])
                    if t == 8:
                        ph.act(os_, os_[:, cj, :], qn[:], AF.Identity, [qn, qkn], scale=nsc)
                        if kind == "k":
                            ptf = prr[kk]
                            ph.tsm(kf, kf[:], qn[:], nsc, [qn, qkn])
                            for j in range(4):
                                ph.tr(ptf, ptf[:, j * 128:(j + 1) * 128], kf[:, j * 128:(j + 1) * 128], identf, [kf, m])
                            ph.cp("dve", kout, kout[:].rearrange("p j d -> p (j d)"), ptf[:], [ptf])
                            for s_ in range(2):
                                ph.store("pool", kout,
                                         self.nck[s_, l].rearrange("(j p) c -> p j c", p=128)[:, :, cj * 128:(cj + 1) * 128],
                                         kout[:, 2 * s_:2 * s_ + 2, :])
                    else:
                        ph.act(qs, qs[:], qn[:], AF.Identity, [qn, qkn], scale=nsc)

                def C(u):
                    t, cj = units[u]
                    kk = u % 2
                    qs, t1, t2, pr = qss[kk], t1s[kk], t2s[kk], prr[kk]
                    os_ = ostg[t % 2]
                    if t < 8:
                        cs_ = cs[t % 2]
                        ph.mm(pr, pr[:], prot[:], qs[:], True, True, [prot, qs])
                        ph.tt(t1, t1[:], qs[:], cs_[:, 0, :], ALU.mult, [qs, cs_])
                        ph.tt(t2, t2[:], pr[:], cs_[:, 1, :], ALU.mult, [pr, cs_])
                        ph.tt(os_, os_[:, cj, :], t1[:], t2[:], ALU.add, [t1, t2], eng="pool")
                    if cj == nch - 1:
                        tsl = slice(t * 512, (t + 1) * 512)
                        if kind == "q":
                            ph.store("pool", os_, self.qT[cb0:cb0 + 4, :, tsl].rearrange("c p t -> p c t"), os_[:])
                        else:
                            ph.store("pool", os_, self.kT[:, :, tsl].rearrange("c p t -> p c t"), os_[:, 0:2, :])

                for u in range(n + 2):
                    if u < n:
                        A(u)
                    if 0 <= u - 1 < n:
                        B(u - 1)
                    if 0 <= u - 2 < n:
                        C(u - 2)

            def mm_slab(si):
                kind, c0, nch, cb0 = slabs[si]
                if kind in ("q", "k"):
                    return qk_slab(si)
                w_ = ws[si % 2]
                if kind == "xbc":
                    xpar[si] = state["x"] % 2
                    state["x"] += 1
                for t in range(NT):
                    ht = hres[t]
                    os_ = ostg[t % 2] if kind != "xbc" else None
                    for cj in range(nch):
                        p_ = pc[cj] if kind != "xbc" else pc[(2 * t + cj) % 4]
                        for kc in range(8):
                            ph.mm(p_, p_[:], w_[:, kc, cj * 128:(cj + 1) * 128], ht[:, kc, :], kc == 0, kc == 7, [w_, ht])
                        if kind == "xbc":
                            self.evac_raw(ph, raws[2 * xpar[si] + cj], p_, t)
                        else:
                            ph.act(os_, os_[:, cj, :], p_[:], AF.Sigmoid, [p_])
                    tsl = slice(t * 512, (t + 1) * 512)
                    if kind == "g":
                        ph.store("pool", os_, self.GA[cb0:cb0 + 4, :, tsl].rearrange("c p t -> p c t"), os_[:])
                    for _ in range(2):
                        if pend:
                            p = pend.pop(0)
                            if p is not None:
                                p()

            def post_slab(si):
                kind, c0, nch, cb0 = slabs[si]
                if kind != "xbc":
                    return
                convs, trs = [], []
                for cj in range(nch):
                    ch = cb0 + cj
                    raw = raws[2 * xpar[si] + cj]
                    cvo = cvos[cj % 2]
                    cp_, tp_ = [], []
                    for (a0, hn) in ((0, HN0), (HN0, RAWN - 2 - HN0)):
                        def conv_piece(raw=raw, ch=ch, a0=a0, hn=hn, cvo=cvo):
                            self.conv_chunk(ph, raw, acc, cw, None, [cw[:, ch, i:i + 1] for i in range(3)], a0=a0, n=hn)
                            ph.act(cvo, cvo[:, a0:a0 + hn], acc[:, 0:hn], AF.Silu, [acc, cb], bias=cb[:, ch:ch + 1])
                        cp_.append(conv_piece)
                    if ch >= 16:
                        def ft_piece(ch=ch, cvo=cvo):
                            dst = self.BT if ch < 20 else self.CT
                            gi = (ch - 16) % 4
                            for (tok0, a0, n) in SEGS:
                                ph.store("pool", cvo, dst[gi, :, tok0:tok0 + n], cvo[:, a0:a0 + n])
                        tp_.append(ft_piece)
                    if ch < 20:
                        if ch < 16:
                            dv = self.XS.rearrange("(b p) c -> p b c", p=128)[:, :, ch * 128:(ch + 1) * 128]
                        else:
                            dv = self.Bt.rearrange("(b p) c -> p b c", p=128)[:, :, (ch - 16) * 128:(ch - 15) * 128]
                        for b0 in range(0, NB, 12):
                            def tr_piece(b0=b0, dv=dv, cvo=cvo):
                                stg = stgs[state["st"] % 2]
                                state["st"] += 1
                                for b1 in range(0, 12, 6):
                                    ptb = ptbs[state["tb"] % 2]
                                    state["tb"] += 1
                                    for bi in range(6):
                                        blk = b0 + b1 + bi
                                        tok = blk * 128
                                        for (tok0, a0, n) in SEGS:
                                            if tok0 <= tok < tok0 + n:
                                                a = a0 + tok - tok0
                                        ph.tr(ptb, ptb[:, bi * 128:(bi + 1) * 128], cvo[:, a:a + 128], identb, [cvo, mb])
                                    ph.cp("act", stg, stg[:, b1:b1 + 6, :].rearrange("p b c -> p (b c)"), ptb[:, 0:6 * 128], [ptb])
                                ph.store("pool", stg, dv[:, b0:b0 + 12, :], stg[:])
                            tp_.append(tr_piece)
                    convs.append(cp_)
                    trs.append(tp_)
                left = state.get("left", [])
                left = (left + [None] * 4)[:max(4, len(left))]
                cur = convs[0] + left + convs[1] + trs[0]
                assert len(cur) <= 18, len(cur)
                cur = cur + [None] * (18 - len(cur))
                pend.extend(cur)
                state["left"] = trs[1]

            n = len(slabs)
            load_slab(0)
            if n > 1:
                load_slab(1)
            mm_slab(0)
            for si in range(n):
                if si + 2 < n:
                    load_slab(si + 2)
                post_slab(si)
                if si + 1 < n:
                    mm_slab(si + 1)
            pend.extend(state.get("left", []))
            while pend:
                p = pend.pop(0)
                if p is not None:
                    p()

    def ph_attn(self, l):
        with Phase(self, f"attn{l}") as ph:
            c = self.consts(ph, ["mask"])
            m = c["mask"]
            identf = m[:, 4, :]
            onesf = m[:, 5, :]
            kall = ph.sb("kall", [128, 2, 4352], BF16)
            vall = ph.sb("vall", [128, 34, 256], BF16)
            ck = ph.sb("ck", [128, 2, 256], F32)
            NQ = 3
            qb_ = [ph.sb(f"qb{i}", [128, 4, 128], BF16) for i in range(NQ)]
            NE = 5
            E = [ph.sb(f"E{i}", [128, 512], BF16) for i in range(NE)]
            NS = 4
            pS = [ph.ps(f"pS{i}", [128, 512]) for i in range(NS)]
            pO = [ph.ps(f"pO{i}", [128, 512]) for i in range(2)]
            pD = [ph.ps(f"pD{i}", [128, 512]) for i in range(2)]
            pT = pS[0]
            accs = [ph.sb(f"acc{i}", [128, 512], F32) for i in range(2)]
            accp = [ph.sb(f"accp{i}", [128, 512], F32) for i in range(2)]
            rec = ph.sb("rec", [128, 512], F32)
            ao = [ph.sb(f"ao{i}", [128, 4, 128], BF16) for i in range(2)]
            scale = 128.0 ** -0.5
            LA = 3
            blocks = []
            for (tok0, ntok, kind) in SEQS:
                nk = 34 if kind == 0 else 2
                for g in range(2):
                    for qb in range(ntok // 128):
                        blocks.append(dict(tok0=tok0, kind=kind, nk=nk, g=g, q0=tok0 + qb * 128, idx=len(blocks)))
            items = [(b, kt) for b in blocks for kt in range(b["nk"])]
            loaded_seq = [None]

            def load_seq(kind, tok0):
                if loaded_seq[0] == kind:
                    return
                loaded_seq[0] = kind
                if kind == 0:
                    ph.load("sp", ck, ck[:], self.cache_k[l].rearrange("(j p) c -> p j c", p=128))
                    for g in range(2):
                        for j in range(2):
                            ph.tr(pT, pT[:, (2 * g + j) * 128:(2 * g + j + 1) * 128], ck[:, j, g * 128:(g + 1) * 128], identf, [ck, m])
                        ph.cp("dve", kall, kall[:, g, 0:256], pT[:, 2 * g * 128:(2 * g + 2) * 128], [pT])
                    ph.load("sp", kall, kall[:, :, 256:4352], self.kT[:, :, 0:4096].rearrange("g p t -> p g t"))
                    ph.load("pool", vall, vall[:, 0:2, :], self.cache_v[l].rearrange("(j p) c -> p j c", p=128))
                    for j0 in range(0, 32, 8):
                        ph.load("sp", vall, vall[:, 2 + j0:2 + j0 + 8, :],
                                self.Vt[j0 * 128:(j0 + 8) * 128, :].rearrange("(j p) c -> p j c", p=128))
                else:
                    ph.load("sp", kall, kall[:, :, 0:256], self.kT[:, :, tok0:tok0 + 256].rearrange("g p t -> p g t"))
                    ph.load("sp", vall, vall[:, 0:2, :], self.Vt[tok0:tok0 + 256, :].rearrange("(j p) c -> p j c", p=128))

            qloaded = set()

            def load_q(b):
                if b["idx"] in qloaded:
                    return
                qloaded.add(b["idx"])
                q_ = qb_[b["idx"] % NQ]
                g, q0 = b["g"], b["q0"]
                ph.load("sp", q_, q_[:], self.qT[4 * g:4 * g + 4, :, q0:q0 + 128].rearrange("h p t -> p h t"))

            def issue_S(i):
                b, kt = items[i]
                load_seq(b["kind"], b["tok0"])
                load_q(b)
                q_ = qb_[b["idx"] % NQ]
                ps_ = pS[i % NS]
                ph.mm(ps_, ps_[:], kall[:, b["g"], kt * 128:(kt + 1) * 128], q_[:].rearrange("p h t -> p (h t)"), True, True, [kall, q_])

            def finalize(b):
                bi = b["idx"]
                po, pd, a_, acc, acp = pO[bi % 2], pD[bi % 2], ao[bi % 2], accs[bi % 2], accp[bi % 2]
                g, q0 = b["g"], b["q0"]
                ph.mm(pd, pd[:], onesf, acc[:], True, True, [m, acc])
                ph.rsqrt(rec, rec[:], pd[:], [pd], power=-1.0)
                ph.tt(a_, a_[:].rearrange("p h t -> p (h t)"), po[:], rec[:], ALU.mult, [po, rec])
                ph.store("pool", a_, self.AT[4 * g:4 * g + 4, :, q0:q0 + 128].rearrange("h p t -> p h t"), a_[:])

            pending = []
            gi = 0
            for (stok0, sntok, skind) in SEQS:
                sidx = [i for i, (b, kt) in enumerate(items) if b["kind"] == skind]
                lo, hi = sidx[0], sidx[-1] + 1
                for i in range(lo, min(lo + LA, hi)):
                    issue_S(i)
                for i in range(lo, hi):
                    b, kt = items[i]
                    if i + LA < hi:
                        issue_S(i + LA)
                    bi = b["idx"]
                    ps_, e_ = pS[i % NS], E[i % NE]
                    po, acc = pO[bi % 2], accs[bi % 2]
                    nk, g = b["nk"], b["g"]
                    ph.act(e_, e_[:], ps_[:], AF.Exp, [ps_], scale=scale)
                    ph.mm(po, po[:], vall[:, kt, g * 128:(g + 1) * 128], e_[:], kt == 0, kt == nk - 1, [vall, e_])
                    if False:
                        a2, eng2 = accp[bi % 2], "pool"
                    else:
                        a2, eng2 = acc, "dve"
                    if kt == 0:
                        ph.cp(eng2, a2, a2[:], e_[:], [e_])
                    else:
                        ph.tt(a2, a2[:], a2[:], e_[:], ALU.add, [a2, e_], eng=eng2)
                    if kt == nk - 1:
                        pending.append((i + 2, b))
                    while pending and pending[0][0] <= i:
                        finalize(pending.pop(0)[1])
                while pending:
                    finalize(pending.pop(0)[1])

    def ph_ssd(self, l, d):
        with Phase(self, f"ssd{l}_{d}") as ph:
            c = self.consts(ph, ["mask"])
            m = c["mask"]
            LE, GE, GT, LT, IDN, ONES = [m[:, i, :] for i in range(6)]
            negA = ph.sb("negA", [128, 64], F32)
            ph.load("sp", negA, negA[:], self.a_log[l:l + 1, :].partition_broadcast(128))
            ph.act(negA, negA[:], negA[:], AF.Exp, [negA])
            ph.op("dve", lambda e: e.tensor_scalar_mul(out=negA[:], in0=negA[:], scalar1=-1.0), [negA], [negA])
            Hs = [ph.sb(f"H{i}", [128, 512], F32) for i in range(4)]
            Hbs = [ph.sb(f"Hb{i}", [128, 512], BF16) for i in range(4)]
            st = ph.sb("st", [128, 16, 128], F32)
            xs = [ph.sb(f"xs{i}", [128, 2048], BF16) for i in range(2)]
            bt = [ph.sb(f"bt{i}", [128, 512], BF16) for i in range(2)]
            CTt = [ph.sb(f"CTt{i}", [128, 4, 128], BF16) for i in range(2)]
            dt = [ph.sb(f"dt{i}", [128, 64], F32) for i in range(2)]
            adts = [ph.sb(f"adt{i}", [128, 64], F32) for i in range(2)]
            css = [ph.sb(f"cs{i}", [128, 192], F32) for i in range(2)]
            for cs in css:
                ph.op("dve", lambda e, cs=cs: e.memset(cs[:], 0.0), [], [cs])
            ecs = [ph.sb(f"ec{i}", [128, 192], F32) for i in range(2)]
            dtes = [ph.sb(f"dte{i}", [128, 32], F32) for i in range(2)]
            xws = [ph.sb(f"xw{i}", [128, 2048], BF16) for i in range(2)]
            pA = ph.ps("pA", [128, 512])
            pc = Tl(ph, pA.t, "pc")
            pYos = [ph.ps(f"pYo{i}", [128, 512]) for i in range(1 if d == 0 else 2)]
            pHs = [ph.ps(f"pH{i}", [128, 512]) for i in range(1 if d == 0 else 2)]
            tmps = [ph.sb(f"tmp{i}", [128, 512], F32) for i in range(2)]
            pS = [ph.ps(f"pS{i}", [128, 512]) for i in range(3 if d == 0 else 1)]
            pT = pS[0]
            if d == 0:
                dsk = ph.sb("dsk", [128, 32], F32)
                ph.load("sp", dsk, dsk[:], self.d_skip[l:l + 1, :].partition_broadcast(128))
                BTt = [ph.sb(f"BTt{i}", [128, 4, 128], BF16) for i in range(2)]
                xdtFs = [ph.sb(f"xdtF{i}", [128, 2048], BF16) for i in range(2)]
                xdtBs = [ph.sb(f"xdtB{i}", [128, 2048], BF16) for i in range(2)]
                pGs = [Tl(ph, pA.t, f"pG{i}") for i in range(3)]
                pYs = [ph.ps(f"pY{i}", [128, 512]) for i in range(2)]
                Gms = [ph.sb(f"Gm{i}", [128, 2, 128], BF16) for i in range(3)]
                Lrs = [ph.sb(f"Lr{i}", [128, 2, 8, 128], F32) for i in range(3)]
                Em = [ph.sb(f"Em{i}", [128, 4, 128], BF16) for i in range(3)]
                Wms = [ph.sb(f"Wm{i}", [128, 2, 8, 128], BF16) for i in range(2)]
                tmp2s = [ph.sb(f"tmp2{i}", [128, 512], F32) for i in range(2)]
                y1 = [ph.sb(f"y1{i}", [128, 2048], F32) for i in range(2)]
            else:
                y1 = [ph.sb(f"y1{i}", [128, 2048], F32) for i in range(3)]
                zs = [ph.sb(f"zs{i}", [128, 2048], BF16) for i in range(2)]
                nw = ph.sb("nw", [128, 2048], F32)
                ph.load("sp", nw, nw[:], self.ssd_norm[l:l + 1, :].partition_broadcast(128))
                mb = ph.sb("identb", [128, 128], BF16)
                ph.op("dve", lambda e: e.tensor_copy(out=mb[:], in_=IDN), [m], [mb])
                junk = ph.sb("junk", [128, 2048], BF16)
                sss = [ph.sb(f"ss{i}", [128, 2], F32) for i in range(2)]
                yns = [ph.sb(f"yn{i}", [128, 2048], BF16) for i in range(2)]
                ptb = ph.ps("ptb", [128, 1024], BF16)
                ynT = [ph.sb(f"ynT{i}", [128, 16, 128], BF16) for i in range(2)]
            ci = 0
            gi = 0
            ri = 0
            for (tok0, ntok, kind) in SEQS:
                if kind == 0:
                    ph.load("sp", st, st[:], self.state[l, d].rearrange("(j p) n -> p j n", p=128))
                    for j in range(16):
                        ph.tr(pT, pT[:, (j % 4) * 128:(j % 4 + 1) * 128], st[:, j, :], IDN, [st, m])
                        if j % 4 == 3:
                            ph.cp("dve", Hs[j // 4], Hs[j // 4][:], pT[:], [pT])
                else:
                    for g in range(4):
                        ph.op("dve", lambda e, g=g: e.memset(Hs[g][:], 0.0), [], [Hs[g]])
                for g in range(4):
                    ph.cp("act", Hbs[g], Hbs[g][:], Hs[g][:], [Hs[g]])
                nch = ntok // 128
                order = list(range(nch)) if d == 0 else list(range(nch - 1, -1, -1))

                def make_chunk(cc):
                    nonlocal ci
                    blk = tok0 // 128 + cc
                    r0 = blk * 128
                    i2 = ci % 2
                    y_ = y1[ci % len(y1)]
                    ci += 1
                    xs_, bt_, CT_, dt_ = xs[i2], bt[i2], CTt[i2], dt[i2]
                    adt, cs, ec, dte, xw = adts[i2], css[i2], ecs[i2], dtes[i2], xws[i2]
                    ecumF, ecumB, cdF, cdB, edsF, edsB = [ec[:, i * 32:(i + 1) * 32] for i in range(6)]
                    xs3 = xs_[:].rearrange("p (h q) -> p h q", q=64)
                    if d == 0:
                        BT_ = BTt[i2]
                        xdtF, xdtB = xdtFs[i2], xdtBs[i2]
                        ecum, cd = ecumF, cdF
                    else:
                        z_ = zs[i2]
                        ecum, cd = ecumB, cdB

                    def P():
                        ph.load("sp", xs_, xs_[:], self.XS[r0:r0 + 128, :])
                        ph.load("sp", bt_, bt_[:], self.Bt[r0:r0 + 128, :])
                        ph.load("sp", CT_, CT_[:], self.CT[:, :, r0:r0 + 128].rearrange("g p t -> p g t"))
                        ph.load("sp", dt_, dt_[:], self.DT[r0:r0 + 128, :])
                        if d == 0:
                            ph.load("sp", BT_, BT_[:], self.BT[:, :, r0:r0 + 128].rearrange("g p t -> p g t"))
                        else:
                            ph.load("sp", y_, y_[:], self.Y1[r0:r0 + 128, :])
                            ph.load("sp", z_, z_[:], self.Zs[r0:r0 + 128, :])
                        ph.tt(adt, adt[:], dt_[:], negA[:], ALU.mult, [dt_, negA])
                        if d == 0:
                            ph.mm(pc, pA[:, 0:32], LE, adt[:, 0:32], True, True, [m, adt])
                        ph.mm(pc, pA[:, 32:64], GE, adt[:, 32:64], True, True, [m, adt])
                        ph.mm(pc, pA[:, 64:128], ONES, adt[:, 0:64], True, True, [m, adt])
                        lo = 0 if d == 0 else 32
                        ph.cp("dve", cs, cs[:, lo:128], pA[:, lo:128], [pc])
                        if d == 0:
                            ph.tt(cs, cs[:, 128:160], cs[:, 64:96], cs[:, 0:32], ALU.subtract, [cs])
                        ph.tt(cs, cs[:, 160:192], cs[:, 96:128], cs[:, 32:64], ALU.subtract, [cs])
                        ph.act(ec, ec[:, lo:192], cs[:, lo:192], AF.Exp, [cs])
                        if d == 0:
                            ph.tt(xdtF, xdtF[:].rearrange("p (h q) -> p h q", q=64), xs3, bc(dt_[:, 0:32].unsqueeze(2), [128, 32, 64]), ALU.mult, [xs_, dt_])
                            ph.tt(xdtB, xdtB[:].rearrange("p (h q) -> p h q", q=64), xs3, bc(dt_[:, 32:64].unsqueeze(2), [128, 32, 64]), ALU.mult, [xs_, dt_], eng="pool")
                            ph.tt(dte, dte[:], dt_[:, 0:32], edsF, ALU.mult, [dt_, ec])
                        else:
                            ph.tt(dte, dte[:], dt_[:, 32:64], edsB, ALU.mult, [dt_, ec])
                        ph.tt(xw, xw[:].rearrange("p (h q) -> p h q", q=64), xs3, bc(dte[:].unsqueeze(2), [128, 32, 64]), ALU.mult, [xs_, dte])

                    bufs = {}

                    def S1(g):
                        nonlocal gi
                        bufs[g] = gi
                        g3 = gi % 3
                        gi += 1
                        if d != 0:
                            return
                        pG, Gm, Lr = pGs[g3], Gms[g3], Lrs[g3]
                        pGap = pA[:, 128 * (1 + g3):128 * (2 + g3)]
                        ph.mm(pG, pGap, BT_[:, g, :], CT_[:, g, :], True, True, [BT_, CT_])
                        ph.tt(Gm, Gm[:, 0, :], pGap, LE, ALU.mult, [pG, m])
                        ph.tt(Gm, Gm[:, 1, :], pGap, GE, ALU.mult, [pG, m])
                        for di, (U, R) in enumerate(((GT, LE), (LT, GE))):
                            ph.tt(Lr, Lr[:, di, :, :], bc(R.unsqueeze(1), [128, 8, 128]),
                                  bc(adt[:, 32 * di + 8 * g:32 * di + 8 * g + 8].unsqueeze(2), [128, 8, 128]), ALU.mult, [m, adt],
                                  eng="pool" if di else "dve")

                    def S2(g):
                        nonlocal ri
                        if d != 0:
                            return
                        gp, g3 = bufs[g] % 2, bufs[g] % 3
                        pY, Gm, Lr, Wm = pYs[gp], Gms[g3], Lrs[g3], Wms[gp]
                        for di, (U, R) in enumerate(((GT, LE), (LT, GE))):
                            for half in range(2):
                                p_, e_ = pS[ri % 3], Em[ri % 3]
                                ri += 1
                                ph.mm(p_, p_[:], U, Lr[:, di, 4 * half:4 * half + 4, :].rearrange("p h t -> p (h t)"), True, True, [m, Lr])
                                ph.act(e_, e_[:].rearrange("p h t -> p (h t)"), p_[:], AF.Exp, [p_])
                                ph.tt(Wm, Wm[:, di, 4 * half:4 * half + 4, :], e_[:], bc(Gm[:, di, :].unsqueeze(1), [128, 4, 128]), ALU.mult, [e_, Gm],
                                      eng="pool" if (di == 1 and half == 1) else "dve")
                        for h in range(8):
                            col = (8 * g + h) * 64
                            ph.mm(pY, pY[:, h * 64:(h + 1) * 64], Wm[:, 0, h, :], xdtF[:, col:col + 64], True, False, [Wm, xdtF])
                            ph.mm(pY, pY[:, h * 64:(h + 1) * 64], Wm[:, 1, h, :], xdtB[:, col:col + 64], False, True, [Wm, xdtB])

                    def S3(g):
                        gp = bufs[g] % 2
                        tmp = tmps[gp]
                        pYo, pH = pYos[gp % len(pYos)], pHs[gp % len(pHs)]
                        gsl = slice(g * 512, (g + 1) * 512)
                        hsl = slice(8 * g, 8 * g + 8)
                        H, Hb = Hs[g], Hbs[g]
                        ph.mm(pYo, pYo[:], CT_[:, g, :], Hb[:], True, True, [CT_, Hb])
                        ph.tt(tmp, tmp[:].rearrange("p (h q) -> p h q", q=64), pYo[:].rearrange("p (h q) -> p h q", q=64),
                              bc(ecum[:, hsl].unsqueeze(2), [128, 8, 64]), ALU.mult, [pYo, ec])
                        if d == 0:
                            pY, tmp2 = pYs[gp], tmp2s[gp]
                            ph.tt(y_, y_[:, gsl], tmp[:], pY[:], ALU.add, [tmp, pY])
                            ph.tt(tmp2, tmp2[:].rearrange("p (h q) -> p h q", q=64), xs_[:, gsl].rearrange("p (h q) -> p h q", q=64),
                                  bc(dsk[:, hsl].unsqueeze(2), [128, 8, 64]), ALU.mult, [xs_, dsk], eng="pool")
                            ph.tt(y_, y_[:, gsl], y_[:, gsl], tmp2[:], ALU.add, [y_, tmp2], eng="pool")
                        else:
                            ph.tt(y_, y_[:, gsl], y_[:, gsl], tmp[:], ALU.add, [y_, tmp], eng="pool")
                        ph.mm(pH, pH[:], bt_[:, g * 128:(g + 1) * 128], xw[:, gsl], True, True, [bt_, xw])
                        ph.tt(H, H[:].rearrange("p (h q) -> p h q", q=64), H[:].rearrange("p (h q) -> p h q", q=64),
                              bc(cd[:, hsl].unsqueeze(2), [128, 8, 64]), ALU.mult, [H, ec])
                        ph.tt(H, H[:], H[:], pH[:], ALU.add, [H, pH])
                        ph.cp("act", Hb, Hb[:], H[:], [H])

                    def E():
                        ph.store("pool", y_, self.Y1[r0:r0 + 128, :], y_[:])

                    def E1():
                        ss = sss[i2]
                        ph.tt(y_, y_[:], y_[:], z_[:], ALU.mult, [y_, z_])
                        ph.act(junk, junk[:], y_[:], AF.Square, [y_], accum=ss[:, 0:1], extra_wr=[ss.r])
                        ph.rsqrt(ss, ss[:, 1:2], ss[:, 0:1], [ss], scale=1.0 / 2048.0)

                    def E2():
                        ss, yn = sss[i2], yns[i2]
                        ph.stt(yn, yn[:], y_[:], ss[:, 1:2], nw[:], ALU.mult, ALU.mult, [y_, ss, nw])
                        yt_ = ynT[i2]
                        for j0 in (0, 8):
                            for j in range(8):
                                ph.tr(ptb, ptb[:, j * 128:(j + 1) * 128], yn[:, (j0 + j) * 128:(j0 + j + 1) * 128], mb[:], [yn, mb])
                            ph.cp("act", yt_, yt_[:, j0:j0 + 8, :].rearrange("p j t -> p (j t)"), ptb[:], [ptb])
                        ph.store("pool", yt_, self.yT[:, :, r0:r0 + 128].rearrange("j p t -> p j t"), yt_[:])

                    return dict(P=P, S1=S1, S2=S2, S3=S3, E=E, E1=E1 if d else None, E2=E2 if d else None)

                chunks = {}

                def get(k):
                    if k not in chunks:
                        chunks[k] = make_chunk(order[k])
                    return chunks[k]

                units = [(k, g) for k in range(nch) for g in range(4)]
                LAG = 1
                get(0)["P"]()
                for j in range(min(LAG, len(units))):
                    get(units[j][0])["S1"](units[j][1])
                for i, (k, g) in enumerate(units):
                    if g == 1 and k + 1 < nch:
                        get(k + 1)["P"]()
                    if i + LAG < len(units):
                        k2, g2 = units[i + LAG]
                        get(k2)["S1"](g2)
                    get(k)["S2"](g)
                    get(k)["S3"](g)
                    if d == 0:
                        if g == 3:
                            get(k)["E"]()
                    else:
                        if g == 0 and k >= 1:
                            get(k - 1)["E1"]()
                        if g == 2 and k >= 1:
                            get(k - 1)["E2"]()
                    if g == 3 and k - 2 in chunks:
                        del chunks[k - 2]
                if d == 1:
                    get(nch - 1)["E1"]()
                    get(nch - 1)["E2"]()
                if kind != 0:
                    s = kind - 1
                    for j in range(16):
                        ph.tr(pT, pT[:, (j % 4) * 128:(j % 4 + 1) * 128], Hs[j // 4][:, (j % 4) * 128:(j % 4 + 1) * 128], IDN, [Hs[j // 4], m])
                        if j % 4 == 3:
                            ph.cp("dve", st, st[:, j - 3:j + 1, :].rearrange("p j n -> p (j n)"), pT[:], [pT])
                    ph.store("pool", st, self.nss[s, l, d].rearrange("(j p) n -> p j n", p=128), st[:])

    def ph_merge(self, l):
        with Phase(self, f"merge{l}") as ph:
            mod = ph.sb("mod", [128, 4, 48, 2], F32)
            ph.load("sp", mod, mod[:], self.MOD)
            wao = ph.sb("wao", [128, 8, 1024], BF16)
            wso = ph.sb("wso", [128, 16, 1024], BF16)
            wo = ph.sb("wo", [128, 8, 1024], BF16)
            self.load_w(ph, wao, lambda o, n: wao[:, :, o:o + n], self.w_attn_o[l], 0, 1024)
            for hf in range(2):
                ph.load("pool", wso, wso[:, 8 * hf:8 * hf + 8, 0:512], self.w_ssd_o[l, 1024 * hf:1024 * (hf + 1), 0:512].rearrange("(c p) n -> p c n", p=128))
                ph.load("pool", wso, wso[:, 8 * hf:8 * hf + 8, 512:1024], self.w_ssd_o[l, 1024 * hf:1024 * (hf + 1), 512:1024].rearrange("(c p) n -> p c n", p=128))
            self.load_w(ph, wo, lambda o, n: wo[:, :, o:o + n], self.w_out[l], 0, 1024)
            at = ph.sb("at", [128, 8, 512], BF16)
            yt = ph.sb("yt", [128, 16, 512], BF16)
            gg = ph.sb("gg", [128, 16, 512], BF16)
            x_ = ph.sb("x", [128, 8, 512], F32)
            mm_ = ph.sb("m", [128, 8, 512], BF16)
            m1 = ph.sb("m1", [128, 512], F32)
            m2 = ph.sb("m2", [128, 512], F32)
            pa = [ph.ps(f"pa{i}", [128, 512]) for i in range(2)]
            pss = [ph.ps(f"pss{i}", [128, 512]) for i in range(2)]
            po = [ph.ps(f"po{i}", [128, 512]) for i in range(2)]
            nb = self.norm_bufs(ph)
            for t in range(NT):
                cond = 0 if t < 8 else 1
                tsl = slice(t * 512, (t + 1) * 512)
                ph.load("sp", at, at[:], self.AT[:, :, tsl].rearrange("c p t -> p c t"))
                ph.load("sp", yt, yt[:, 0:8, :], self.yT[0:8, :, tsl].rearrange("c p t -> p c t"))
                ph.load("sp", yt, yt[:, 8:16, :], self.yT[8:16, :, tsl].rearrange("c p t -> p c t"))
                ph.load("sp", gg, gg[:, 0:8, :], self.GA[0:8, :, tsl].rearrange("c p t -> p c t"))
                ph.load("sp", gg, gg[:, 8:16, :], self.GA[8:16, :, tsl].rearrange("c p t -> p c t"))
                ph.load("sp", x_, x_[:], self.xT[:, :, tsl].rearrange("c p t -> p c t"))
                for j in range(8):
                    pa_, ps_ = pa[j % 2], pss[j % 2]
                    for k in range(8):
                        ph.mm(pa_, pa_[:], wao[:, k, j * 128:(j + 1) * 128], at[:, k, :], k == 0, k == 7, [wao, at])
                    for k in range(16):
                        ph.mm(ps_, ps_[:], wso[:, k, j * 128:(j + 1) * 128], yt[:, k, :], k == 0, k == 15, [wso, yt])
                    ph.tt(m1, m1[:], pa_[:], gg[:, j, :], ALU.mult, [pa_, gg])
                    ph.tt(m2, m2[:], ps_[:], gg[:, 8 + j, :], ALU.mult, [ps_, gg])
                    ph.tt(mm_, mm_[:, j, :], m1[:], m2[:], ALU.add, [m1, m2])
                for j2 in range(8):
                    po_ = po[j2 % 2]
                    for j in range(8):
                        ph.mm(po_, po_[:], wo[:, j, j2 * 128:(j2 + 1) * 128], mm_[:, j, :], j == 0, j == 7, [wo, mm_])
                    ph.stt(x_, x_[:, j2, :], po_[:], mod[:, l, 16 + j2, cond:cond + 1], x_[:, j2, :], ALU.mult, ALU.add, [po_, mod, x_])
                ph.store("pool", x_, self.xT[:, :, tsl].rearrange("c p t -> p c t"), x_[:])
                self.emit_norm(ph, nb, mod, x_, l, 1, t)

    def ph_ffn_up(self, l):
        with Phase(self, f"up{l}") as ph:
            cw = ph.sb("cw", [128, 44, 3], F32)
            ph.load("sp", cw, cw[:], self.ffn_cw[:, l, :, :])
            cb = ph.sb("cb", [128, 44], F32)
            ph.load("sp", cb, cb[:], self.ffn_cb[:, l, :])
            ws = [ph.sb(f"ws{i}", [128, 8, 256], BF16) for i in range(3)]
            hres = [ph.sb(f"hr{i}", [128, 8, 512], BF16) for i in range(NT)]
            for t in range(NT):
                ph.load("sp", hres[t], hres[t][:], self.hT[:, :, t * 512:(t + 1) * 512].rearrange("c p t -> p c t"))
            pc = [ph.ps(f"pc{i}", [128, 512]) for i in range(6)]
            raws = [ph.sb(f"raw{i}", [128, RAWN], F32) for i in range(4)]
            for r_ in raws:
                ph.op("pool", lambda e, r_=r_: e.memset(r_[:], 0.0), [], [r_])
            HN = (RAWN - 2) // 2
            accv = ph.sb("accv", [128, HN], F32)
            accg = ph.sb("accg", [128, HN], F32)
            aos = [ph.sb(f"ao{i}", [128, RAWN], BF16) for i in range(2)]
            n = RAWN - 2
            NS = 22
            state = dict(h=0, p=0)

            def load_slab(i):
                w_ = ws[i % 3]
                ph.load("pool", w_, w_[:, :, 0:128], self.w_up[l][:, 128 * i:128 * (i + 1)].rearrange("(c p) n -> p c n", p=128))
                ph.load("pool", w_, w_[:, :, 128:256], self.w_up[l][:, 2816 + 128 * i:2816 + 128 * (i + 1)].rearrange("(c p) n -> p c n", p=128))

            def mm_slab(si):
                w_ = ws[si % 3]
                for t in range(NT):
                    ht = hres[t]
                    for cj in range(2):
                        p_ = pc[state["p"] % 6]
                        state["p"] += 1
                        for kc in range(8):
                            ph.mm(p_, p_[:], w_[:, kc, cj * 128:(cj + 1) * 128], ht[:, kc, :], kc == 0, kc == 7, [w_, ht])
                        self.evac_raw(ph, raws[2 * (si % 2) + cj], p_, t, eng="act")

            def post_slab(si):
                chv = si
                chg = 22 + si
                ao = aos[si % 2]
                rv, rg = raws[2 * (si % 2)], raws[2 * (si % 2) + 1]
                for a0 in (0, HN):
                    self.conv_chunk(ph, rg, accg, cw, None, [cw[:, chg, i:i + 1] for i in range(3)], a0=a0, n=HN)
                    ph.act(accg, accg[:, 0:HN], accg[:, 0:HN], AF.Silu, [accg, cb], bias=cb[:, chg:chg + 1])
                    self.conv_chunk(ph, rv, accv, cw, None, [cw[:, chv, i:i + 1] for i in range(3)], a0=a0, n=HN)
                    ph.stt(ao, ao[:, a0:a0 + HN], accv[:, 0:HN], cb[:, chv:chv + 1], accg[:, 0:HN], ALU.add, ALU.mult, [accv, cb, accg])
                for (tok0, a0, nn) in SEGS:
                    ph.store("pool", ao, self.actT[chv, :, tok0:tok0 + nn], ao[:, a0:a0 + nn])

            load_slab(0)
            load_slab(1)
            mm_slab(0)
            for si in range(NS):
                if si + 2 < NS:
                    load_slab(si + 2)
                if si + 1 < NS:
                    mm_slab(si + 1)
                post_slab(si)

    def ph_ffn_down(self, l):
        with Phase(self, f"down{l}") as ph:
            mod = ph.sb("mod", [128, 4, 48, 2], F32)
            ph.load("sp", mod, mod[:], self.MOD)
            wd = ph.sb("wd", [128, 22, 1024], BF16)
            for c0 in range(0, 22, 8):
                ncn = min(8, 22 - c0)
                for hf in range(2):
                    ph.load("pool", wd, wd[:, c0:c0 + ncn, 512 * hf:512 * (hf + 1)],
                            self.w_down[l, c0 * 128:(c0 + ncn) * 128, 512 * hf:512 * (hf + 1)].rearrange("(c p) n -> p c n", p=128))
            acts = [ph.sb(f"act{i}", [128, 22, 512], BF16) for i in range(2)]
            xs = [ph.sb(f"x{i}", [128, 8, 512], F32) for i in range(2)]
            po = [ph.ps(f"po{i}", [128, 512]) for i in range(2)]
            nb = self.norm_bufs(ph) if l + 1 < DEPTH else None
            for t in range(NT):
                cond = 0 if t < 8 else 1
                tsl = slice(t * 512, (t + 1) * 512)
                a_, x_ = acts[t % 2], xs[t % 2]
                ph.load("sp", a_, a_[:, 0:11, :], self.actT[0:11, :, tsl].rearrange("c p t -> p c t"))
                ph.load("sp", a_, a_[:, 11:22, :], self.actT[11:22, :, tsl].rearrange("c p t -> p c t"))
                ph.load("sp", x_, x_[:], self.xT[:, :, tsl].rearrange("c p t -> p c t"))
                for j2 in range(8):
                    po_ = po[j2 % 2]
                    for j in range(22):
                        ph.mm(po_, po_[:], wd[:, j, j2 * 128:(j2 + 1) * 128], a_[:, j, :], j == 0, j == 21, [wd, a_])
                    ph.stt(x_, x_[:, j2, :], po_[:], mod[:, l, 40 + j2, cond:cond + 1], x_[:, j2, :], ALU.mult, ALU.add, [po_, mod, x_])
                ph.store("pool", x_, self.xT[:, :, tsl].rearrange("c p t -> p c t"), x_[:])
                if nb is not None:
                    self.emit_norm(ph, nb, mod, x_, l + 1, 0, t)

    def ph_final(self):
        with Phase(self, "final") as ph:
            c = self.consts(ph, ["mask"])
            m = c["mask"]
            ident = m[:, 4, :]
            xs = [ph.sb(f"x{i}", [128, 8, 512], F32) for i in range(2)]
            ot = [ph.sb(f"ot{i}", [128, 1024], F32) for i in range(2)]
            pt = [ph.ps(f"pt{i}", [128, 512]) for i in range(4)]
            k = 0
            for t in range(NT):
                x_ = xs[t % 2]
                ph.load("sp", x_, x_[:], self.xT[:, :, t * 512:(t + 1) * 512].rearrange("c p t -> p c t"))
                for j in range(4):
                    o_ = ot[k % 2]
                    k += 1
                    for hf in range(2):
                        p_ = pt[(2 * k + hf) % 4]
                        for cc in range(4):
                            cch = 4 * hf + cc
                            ph.tr(p_, p_[:, cc * 128:(cc + 1) * 128], x_[:, cch, j * 128:(j + 1) * 128], ident, [x_, m])
                        ph.cp("act" if hf else "dve", o_, o_[:, hf * 512:(hf + 1) * 512], p_[:], [p_])
                    r0 = (4 * t + j) * 128
                    ph.store("pool", o_, self.y_tok[r0:r0 + 128, :], o_[:])


_CACHE = {}


def _consts():
    half = 64
    inv_freq = (10000.0 ** (-np.arange(0, half, 2, dtype=np.float32) / half)).astype(np.float32)
    pos = np.arange(4096)
    row = (pos // 64).astype(np.float32)
    col = (pos % 64).astype(np.float32)
    ang_r = row[:, None] * inv_freq
    ang_c = col[:, None] * inv_freq
    ang = np.concatenate([ang_r, ang_r, ang_c, ang_c], axis=-1).astype(np.float32)
    rope = np.stack([np.cos(ang).T, np.sin(ang).T]).astype(np.float32)
    k = np.arange(128)[:, None]
    j = np.arange(128)[None, :]
    cm = np.stack([(k <= j), (k >= j), (k > j), (k < j), (k == j), np.ones((128, 128), bool)], axis=1).astype(np.float32)
    prot = np.zeros((128, 128), np.float32)
    for d in range(128):
        q = d % 64
        base = d - q
        if q < 32:
            prot[base + q + 32, d] = -1.0
        else:
            prot[base + q - 32, d] = 1.0
    return rope, np.ascontiguousarray(cm), prot


def kernel(x_prompt, x_sample, c, cache_k, cache_v, state_ssd, c_ctx, w_mod, b_mod, w_in, q_norm, k_norm,
           conv_w, conv_b, dt_bias, a_log, d_skip, ssd_norm, w_attn_o, w_ssd_o, w_out, w_up,
           ffn_conv_w, ffn_conv_b, w_down):
    f = lambda a: np.ascontiguousarray(np.asarray(a, dtype=np.float32))
    if "nc" not in _CACHE:
        _CACHE["nc"] = Kernel().nc
    nc = _CACHE["nc"]
    rope, cm, prot = _consts()
    x_prompt, x_sample = f(x_prompt), f(x_sample)
    shared = {
        "w_mod": f(w_mod), "w_in": f(w_in), "w_attn_o": f(w_attn_o), "w_ssd_o": f(w_ssd_o), "w_out": f(w_out),
        "w_up": f(w_up), "w_down": f(w_down),
        "b_mod": f(np.asarray(b_mod).reshape(4, 48, 128).transpose(2, 0, 1)),
        "qkn": f(np.stack([np.asarray(q_norm), np.asarray(k_norm)], axis=-1).transpose(1, 0, 2)),
        "conv_w": f(np.asarray(conv_w).reshape(4, 3, 24, 128).transpose(3, 0, 2, 1)),
        "conv_b": f(np.asarray(conv_b).reshape(4, 24, 128).transpose(2, 0, 1)),
        "dt_bias": f(np.asarray(dt_bias).reshape(4, 64)), "a_log": f(np.asarray(a_log).reshape(4, 64)),
        "d_skip": f(d_skip), "ssd_norm": f(ssd_norm),
        "ffn_cw": f(np.asarray(ffn_conv_w).reshape(4, 3, 44, 128).transpose(3, 0, 2, 1)),
        "ffn_cb": f(np.asarray(ffn_conv_b).reshape(4, 44, 128).transpose(2, 0, 1)),
        "rope": rope, "cmask": cm, "prot": prot,
    }
    in_maps = []
    for core in range(8):
        b = core // 4
        d = dict(shared)
        d["x_tok"] = np.ascontiguousarray(np.concatenate([x_sample[b], x_prompt[2 * core], x_prompt[2 * core + 1]], axis=0))
        cv = np.stack([np.asarray(c)[b], np.asarray(c_ctx)], axis=-1)
        d["cvec"] = f(cv.reshape(8, 128, 2).transpose(1, 0, 2))
        d["cache_k"] = f(np.asarray(cache_k)[b].reshape(4, 256, 256))
        d["cache_v"] = f(np.asarray(cache_v)[b].reshape(4, 256, 256))
        d["state"] = f(np.asarray(state_ssd)[b].reshape(4, 2, 2048, 128))
        in_maps.append(d)
    res = run_bass_kernel_spmd(nc, in_maps, core_ids=list(range(8)))
    R = res.results
    y_prompt = np.empty((16, 256, 1024), np.float32)
    y_sample = np.empty((2, 4096, 1024), np.float32)
    nk = np.empty((16, 4, 256, 2, 128), np.float32)
    nv = np.empty((16, 4, 256, 2, 128), np.float32)
    ns = np.empty((16, 4, 2, 32, 64, 128), np.float32)
    for core in range(8):
        r = R[core]
        yt = np.asarray(r["y_tok"])
        if core % 4 == 0:
            y_sample[core // 4] = yt[0:4096]
        for s in range(2):
            y_prompt[2 * core + s] = yt[4096 + 256 * s:4096 + 256 * (s + 1)]
            nk[2 * core + s] = np.asarray(r["nck"])[s].reshape(4, 256, 2, 128)
            nv[2 * core + s] = np.asarray(r["ncv"])[s].reshape(4, 256, 2, 128)
            ns[2 * core + s] = np.asarray(r["nss"])[s].reshape(4, 2, 32, 64, 128)
    return (y_prompt, y_sample, nk, nv, ns)
```
